# Optimizing a Trainium2 kernel written in Bass

```python
import jax, jax.numpy as jnp
from jax import lax
import numpy as np

D_MODEL = 1024
BATCH = 16
SEQ = 2048
DEPTH = 2

MEM_TOKENS = 256
EPS = 1e-6
NEG_INF = -1e30
RET_HEADS = 4
RET_QK_DIM = 64
RET_V_DIM = 128
RET_CHUNK = 128
RET_ROT_BASE = 10000.0
GDN_HEADS = 4
GDN_HEAD_DIM = 128
GDN_CONV = 4
GDN_CHUNK = 64
DIL_GROUPS = ((128, 1), (512, 4), (2048, 16))
DIL_HEADS = 4
DIL_HEAD_DIM = 64
DIL_ROT_DIM = DIL_HEAD_DIM // 4
ROPE_THETA = 500000.0
XATTN_HEADS = 4
XATTN_HEAD_DIM = 128
FFN_DIM = 2816
FFN_CONV = 3
N_BRANCH = 3

RET_QK_W = RET_HEADS * RET_QK_DIM
RET_V_W = RET_HEADS * RET_V_DIM
GDN_W = GDN_HEADS * GDN_HEAD_DIM
DIL_W = len(DIL_GROUPS) * DIL_HEADS * DIL_HEAD_DIM
DIL_OUT_W = DIL_HEADS * DIL_HEAD_DIM
XATTN_W = XATTN_HEADS * XATTN_HEAD_DIM
IN_SIZES = (RET_QK_W, RET_QK_W, RET_V_W, RET_V_W,
            GDN_W, GDN_W, GDN_W, GDN_HEADS, GDN_HEADS, GDN_W,
            DIL_W, DIL_W, DIL_W, N_BRANCH * D_MODEL)
IN_COLS = sum(IN_SIZES)

kernel_name = 'hybrid_ret_gdn_dilated_block'


def rms_norm(x, g):
    xf = x.astype(jnp.float32)
    y = xf * lax.rsqrt(jnp.mean(xf * xf, axis=-1, keepdims=True) + EPS)
    return (y * g.astype(jnp.float32)).astype(x.dtype)


def l2_norm(x):
    xf = x.astype(jnp.float32)
    return xf * lax.rsqrt(jnp.sum(xf * xf, axis=-1, keepdims=True) + EPS)


def causal_depthwise_conv(x, w):
    k_width, channels = w.shape
    return lax.conv_general_dilated(
        x, w[:, None, :].astype(x.dtype), window_strides=(1,), padding=((k_width - 1, 0),),
        dimension_numbers=('NWC', 'WIO', 'NWC'), feature_group_count=channels)


def rotate(x, positions, inv_freq):
    ang = positions.astype(jnp.float32)[..., None] * inv_freq
    ang = ang.reshape(ang.shape[:2] + (1,) * (x.ndim - 3) + ang.shape[-1:])
    cos = jnp.cos(ang).astype(x.dtype)
    sin = jnp.sin(ang).astype(x.dtype)
    x1, x2 = jnp.split(x, 2, axis=-1)
    return jnp.concatenate([x1 * cos - x2 * sin, x2 * cos + x1 * sin], axis=-1)


def retention_chunked(q, k, v, log_gamma):
    B, T, H, Dk = q.shape
    Dv = v.shape[-1]
    C = RET_CHUNK
    N = T // C
    f = jnp.float32
    def chunks(z):
        return z.astype(f).reshape(B, N, C, H, -1).transpose(1, 0, 3, 2, 4)
    qc, kc, vc = chunks(q), chunks(k), chunks(v)
    idx = jnp.arange(C, dtype=f)
    diff = idx[:, None] - idx[None, :]
    lg = log_gamma.astype(f)
    decay = jnp.where(diff >= 0, jnp.exp(jnp.maximum(diff, 0.0) * lg[:, None, None]), 0.0)
    intra = jnp.einsum('nbhqd,nbhkd->nbhqk', qc, kc) * decay
    intra_o = jnp.einsum('nbhqk,nbhkv->nbhqv', intra, vc)
    q_decay = jnp.exp((idx + 1.0) * lg[:, None])[:, :, None]
    k_decay = jnp.exp((C - 1.0 - idx) * lg[:, None])[:, :, None]
    chunk_decay = jnp.exp(C * lg)[:, None, None]
    def step(state, inp):
        qi, ki, vi = inp
        inter = jnp.einsum('bhqd,bhdv->bhqv', qi * q_decay, state)
        state = state * chunk_decay + jnp.einsum('bhkd,bhkv->bhdv', ki * k_decay, vi)
        return state, inter
    _, inter_o = lax.scan(step, jnp.zeros((B, H, Dk, Dv), f), (qc, kc, vc))
    o = intra_o + inter_o
    return o.transpose(1, 0, 3, 2, 4).reshape(B, T, H, Dv)


def gated_delta_chunked(q, k, v, g, beta):
    B, T, H, Dk = q.shape
    Dv = v.shape[-1]
    C = GDN_CHUNK
    N = T // C
    f = jnp.float32
    def chunks(z):
        return z.astype(f).reshape(B, N, C, H, -1).transpose(1, 0, 3, 2, 4)
    qc = chunks(q) * Dk ** -0.5
    kc, vc = chunks(k), chunks(v)
    gc = jnp.cumsum(g.astype(f).reshape(B, N, C, H).transpose(1, 0, 3, 2), axis=-1)
    bc = beta.astype(f).reshape(B, N, C, H).transpose(1, 0, 3, 2)
    tri = jnp.tril(jnp.ones((C, C), bool))
    strict = jnp.tril(jnp.ones((C, C), bool), -1)
    diff = gc[..., :, None] - gc[..., None, :]
    decay = jnp.where(tri, jnp.exp(jnp.where(tri, diff, 0.0)), 0.0)
    kb = kc * bc[..., None]
    lower = jnp.where(strict, jnp.einsum('nbhid,nbhjd->nbhij', kb, kc) * decay, 0.0)
    a_mat = lower + jnp.eye(C, dtype=f)
    rhs = jnp.concatenate([vc * bc[..., None], kb * jnp.exp(gc)[..., None]], axis=-1)
    sol = lax.linalg.triangular_solve(a_mat, rhs, left_side=True, lower=True, unit_diagonal=True)
    u, w = sol[..., :Dv], sol[..., Dv:]
    attn = jnp.where(tri, jnp.einsum('nbhid,nbhjd->nbhij', qc, kc) * decay, 0.0)
    def step(S, inp):
        qi, ki, ui, wi, gi, ai = inp
        v_new = ui - jnp.einsum('bhcd,bhdv->bhcv', wi, S)
        o = (jnp.einsum('bhcd,bhdv->bhcv', qi * jnp.exp(gi)[..., None], S)
             + jnp.einsum('bhij,bhjv->bhiv', ai, v_new))
        g_last = gi[..., -1]
        S = (S * jnp.exp(g_last)[..., None, None]
             + jnp.einsum('bhcd,bhcv->bhdv', ki * jnp.exp(g_last[..., None] - gi)[..., None], v_new))
        return S, o
    _, o = lax.scan(step, jnp.zeros((B, H, Dk, Dv), f), (qc, kc, u, w, gc, attn))
    return o.transpose(1, 0, 3, 2, 4).reshape(B, T, H, Dv)


def dilated_window_attention(q, k, v, dilation, span):
    B, T, H, Dh = q.shape
    L = T // dilation
    nb = -(-L // span)
    Lp = nb * span
    def to_blocks(z):
        z = z.reshape(B, L, dilation, H, Dh).transpose(0, 2, 1, 3, 4)
        z = jnp.pad(z, ((0, 0), (0, 0), (0, Lp - L), (0, 0), (0, 0)))
        return z.reshape(B, dilation, nb, span, H, Dh)
    qb, kb, vb = to_blocks(q), to_blocks(k), to_blocks(v)
    def with_prev(z):
        prev = jnp.pad(z, ((0, 0), (0, 0), (1, 0), (0, 0), (0, 0), (0, 0)))[:, :, :-1]
        return jnp.concatenate([prev, z], axis=3)
    kc, vc = with_prev(kb), with_prev(vb)
    s = jnp.einsum('bcnqhd,bcnkhd->bcnhqk', qb, kc).astype(jnp.float32) * Dh ** -0.5
    blk = jnp.arange(nb)[:, None, None]
    iq = jnp.arange(span)[None, :, None]
    ik = jnp.arange(2 * span)[None, None, :]
    dist = span + iq - ik
    valid = (dist >= 0) & (dist <= span) & ((blk > 0) | (ik >= span))
    s = jnp.where(valid[:, None], s, NEG_INF)
    m = jnp.max(s, axis=-1, keepdims=True)
    p = jnp.exp(s - m)
    den = jnp.sum(p, axis=-1)
    lse = jnp.swapaxes(m[..., 0] + jnp.log(den), -1, -2)
    o = jnp.einsum('bcnhqk,bcnkhd->bcnqhd', p, vc.astype(jnp.float32))
    o = o / jnp.swapaxes(den, -1, -2)[..., None]
    o = o.reshape(B, dilation, Lp, H, Dh)[:, :, :L].transpose(0, 2, 1, 3, 4).reshape(B, T, H, Dh)
    lse = lse.reshape(B, dilation, Lp, H)[:, :, :L].transpose(0, 2, 1, 3).reshape(B, T, H)
    return o, lse


def hybrid_mixer(h, positions, w_in, ret_norm_g, gdn_conv_w, gdn_a_log, gdn_dt_bias, gdn_norm_g,
                 w_br_ret, w_br_gdn, w_br_dil, w_out):
    B, T, _ = h.shape
    f = jnp.float32
    split_points = np.cumsum(IN_SIZES)[:-1].tolist()
    (r_q, r_k, r_v, r_g, g_q, g_k, g_v, g_a, g_b, g_z,
     d_q, d_k, d_v, gates) = jnp.split(h @ w_in, split_points, axis=-1)

    ret_inv = 1.0 / (RET_ROT_BASE ** jnp.linspace(0.0, 1.0, RET_QK_DIM // 2, dtype=f))
    q = rotate(r_q.reshape(B, T, RET_HEADS, RET_QK_DIM), positions, ret_inv)
    k = rotate(r_k.reshape(B, T, RET_HEADS, RET_QK_DIM), positions, ret_inv) * RET_QK_DIM ** -0.5
    log_gamma = jnp.log1p(-jnp.exp2(-5.0 - jnp.arange(RET_HEADS, dtype=f)))
    o = retention_chunked(q, k, r_v.reshape(B, T, RET_HEADS, RET_V_DIM), log_gamma)
    o = rms_norm(o, ret_norm_g) * jax.nn.silu(r_g.reshape(B, T, RET_HEADS, RET_V_DIM).astype(f))
    o_ret = o.reshape(B, T, RET_V_W).astype(h.dtype)

    qkv = jax.nn.silu(causal_depthwise_conv(jnp.concatenate([g_q, g_k, g_v], axis=-1), gdn_conv_w))
    c_q, c_k, c_v = jnp.split(qkv, 3, axis=-1)
    q = l2_norm(c_q.reshape(B, T, GDN_HEADS, GDN_HEAD_DIM))
    k = l2_norm(c_k.reshape(B, T, GDN_HEADS, GDN_HEAD_DIM))
    beta = jax.nn.sigmoid(g_b.astype(f))
    log_decay = -jnp.exp(gdn_a_log.astype(f)) * jax.nn.softplus(g_a.astype(f) + gdn_dt_bias.astype(f))
    o = gated_delta_chunked(q, k, c_v.reshape(B, T, GDN_HEADS, GDN_HEAD_DIM), log_decay, beta)
    o = rms_norm(o, gdn_norm_g) * jax.nn.silu(g_z.reshape(B, T, GDN_HEADS, GDN_HEAD_DIM).astype(f))
    o_gdn = o.reshape(B, T, GDN_W).astype(h.dtype)

    n_groups = len(DIL_GROUPS)
    dil_inv = ROPE_THETA ** (-jnp.arange(0, DIL_ROT_DIM, 2, dtype=f) / DIL_ROT_DIM)
    def partial_rope(z):
        z = z.reshape(B, T, n_groups, DIL_HEADS, DIL_HEAD_DIM)
        return jnp.concatenate([rotate(z[..., :DIL_ROT_DIM], positions, dil_inv), z[..., DIL_ROT_DIM:]], axis=-1)
    q, k = partial_rope(d_q), partial_rope(d_k)
    v = d_v.reshape(B, T, n_groups, DIL_HEADS, DIL_HEAD_DIM)
    outs, lses = [], []
    for gi, (window, dilation) in enumerate(DIL_GROUPS):
        o_g, lse_g = dilated_window_attention(q[:, :, gi], k[:, :, gi], v[:, :, gi], dilation, window // dilation)
        outs.append(o_g)
        lses.append(lse_g)
    wts = jax.nn.softmax(jnp.stack(lses, axis=2), axis=2)
    o_dil = jnp.sum(wts[..., None] * jnp.stack(outs, axis=2), axis=2).reshape(B, T, DIL_OUT_W).astype(h.dtype)

    gate = jax.nn.sigmoid(gates.reshape(B, T, N_BRANCH, D_MODEL))
    merged = (gate[:, :, 0] * (o_ret @ w_br_ret) + gate[:, :, 1] * (o_gdn @ w_br_gdn)
              + gate[:, :, 2] * (o_dil @ w_br_dil))
    return merged @ w_out


def memory_cross_attention(h, mem_n, wq, wkv, wo):
    B, T, _ = h.shape
    M = mem_n.shape[1]
    q = (h @ wq).reshape(B, T, XATTN_HEADS, XATTN_HEAD_DIM)
    k, v = jnp.split(mem_n @ wkv, 2, axis=-1)
    k = k.reshape(B, M, XATTN_HEADS, XATTN_HEAD_DIM)
    v = v.reshape(B, M, XATTN_HEADS, XATTN_HEAD_DIM)
    s = jnp.einsum('bthd,bmhd->bhtm', q, k).astype(jnp.float32) * XATTN_HEAD_DIM ** -0.5
    p = jax.nn.softmax(s, axis=-1).astype(v.dtype)
    o = jnp.einsum('bhtm,bmhd->bthd', p, v).reshape(B, T, XATTN_W)
    return o @ wo


def conv_gated_mlp(h, w_up, conv_w, conv_b, w_down):
    up = causal_depthwise_conv(h @ w_up, conv_w) + conv_b
    a, u = jnp.split(up, 2, axis=-1)
    return (jax.nn.silu(a) * u) @ w_down


def setup_inputs(seed: int = 0) -> dict:
    key = jax.random.key(seed)
    ks = iter(jax.random.split(key, 32))
    f = jnp.float32
    L = DEPTH
    def nrm(shape, fan_in):
        return jax.random.normal(next(ks), shape, f) * fan_in ** -0.5
    def gain(shape):
        return 1.0 + 0.02 * jax.random.normal(next(ks), shape, f)
    x = jax.random.normal(next(ks), (BATCH, SEQ, D_MODEL), f)
    mem = jax.random.normal(next(ks), (BATCH, MEM_TOKENS, D_MODEL), f)
    positions = (jax.random.randint(next(ks), (BATCH, 1), 0, 4096, dtype=jnp.int32)
                 + jnp.arange(SEQ, dtype=jnp.int32)[None, :])
    gdn_a_log = jnp.log(jax.random.uniform(next(ks), (L, GDN_HEADS), f, 1.0, 16.0))
    dt = jnp.exp(jax.random.uniform(next(ks), (L, GDN_HEADS), f, float(np.log(1e-3)), float(np.log(1e-1))))
    gdn_dt_bias = dt + jnp.log(-jnp.expm1(-dt))
    return {
        'x': x,
        'mem': mem,
        'positions': positions,
        'norm_mix_g': gain((L, D_MODEL)),
        'w_in': nrm((L, D_MODEL, IN_COLS), D_MODEL),
        'ret_norm_g': gain((L, RET_HEADS, RET_V_DIM)),
        'gdn_conv_w': nrm((L, GDN_CONV, 3 * GDN_W), GDN_CONV),
        'gdn_a_log': gdn_a_log,
        'gdn_dt_bias': gdn_dt_bias,
        'gdn_norm_g': gain((L, GDN_HEAD_DIM)),
        'w_br_ret': nrm((L, RET_V_W, D_MODEL), RET_V_W),
        'w_br_gdn': nrm((L, GDN_W, D_MODEL), GDN_W),
        'w_br_dil': nrm((L, DIL_OUT_W, D_MODEL), DIL_OUT_W),
        'w_out': nrm((L, D_MODEL, D_MODEL), D_MODEL),
        'norm_xattn_g': gain((L, D_MODEL)),
        'norm_mem_g': gain((L, D_MODEL)),
        'xattn_wq': nrm((L, D_MODEL, XATTN_W), D_MODEL),
        'xattn_wkv': nrm((L, D_MODEL, 2 * XATTN_W), D_MODEL),
        'xattn_wo': nrm((L, XATTN_W, D_MODEL), XATTN_W),
        'norm_ffn_g': gain((L, D_MODEL)),
        'ffn_w_up': nrm((L, D_MODEL, 2 * FFN_DIM), D_MODEL),
        'ffn_conv_w': nrm((L, FFN_CONV, 2 * FFN_DIM), FFN_CONV),
        'ffn_conv_b': 0.01 * jax.random.normal(next(ks), (L, 2 * FFN_DIM), f),
        'ffn_w_down': nrm((L, FFN_DIM, D_MODEL), FFN_DIM),
        'final_norm_g': gain((D_MODEL,)),
    }


def reference(x, mem, positions, norm_mix_g, w_in, ret_norm_g, gdn_conv_w, gdn_a_log, gdn_dt_bias,
              gdn_norm_g, w_br_ret, w_br_gdn, w_br_dil, w_out, norm_xattn_g, norm_mem_g, xattn_wq,
              xattn_wkv, xattn_wo, norm_ffn_g, ffn_w_up, ffn_conv_w, ffn_conv_b, ffn_w_down, final_norm_g):
    for l in range(DEPTH):
        x = x + hybrid_mixer(rms_norm(x, norm_mix_g[l]), positions, w_in[l], ret_norm_g[l], gdn_conv_w[l],
                             gdn_a_log[l], gdn_dt_bias[l], gdn_norm_g[l], w_br_ret[l], w_br_gdn[l],
                             w_br_dil[l], w_out[l])
        x = x + memory_cross_attention(rms_norm(x, norm_xattn_g[l]), rms_norm(mem, norm_mem_g[l]),
                                       xattn_wq[l], xattn_wkv[l], xattn_wo[l])
        x = x + conv_gated_mlp(rms_norm(x, norm_ffn_g[l]), ffn_w_up[l], ffn_conv_w[l], ffn_conv_b[l],
                               ffn_w_down[l])
    return rms_norm(x, final_norm_g)
```

```python
import numpy as np
from contextlib import ExitStack
import concourse.bass as bass
import concourse.mybir as mybir
from concourse.bass_utils import run_bass_kernel_spmd

F32 = mybir.dt.float32
BF16 = mybir.dt.bfloat16
I32 = mybir.dt.int32
ALU = mybir.AluOpType
AF = mybir.ActivationFunctionType
AX = mybir.AxisListType


class Sched:
    NDS = 12
    EPOCH = 20000

    def __init__(self, nc, stack):
        self.nc = nc
        self.st = stack
        self.eng = {'pe': nc.tensor, 'act': nc.scalar, 'dve': nc.vector, 'pool': nc.gpsimd, 'sp': nc.sync}
        self.csem = {e: [] for e in ('pe', 'act', 'dve', 'pool')}
        self.cnt = {e: 0 for e in self.csem}
        self.dsem = {q: [stack.enter_context(nc.semaphore(f"d_{q}{i}")) for i in range(self.NDS)]
                     for q in ('sp', 'pool', 'act')}
        self.dval = {q: [0] * self.NDS for q in self.dsem}
        self.drr = {q: 0 for q in self.dsem}
        self.seen = {}
        self.lastw = {}
        self.reads = {}
        self.nwaits = 0
        self.ninst = 0
        self.scratch = None

    def sb(self, name, shape, dt):
        return self.st.enter_context(self.nc.sbuf_tensor(name, list(shape), dt))

    def ps(self, name, shape, dt):
        return self.st.enter_context(self.nc.psum_tensor(name, list(shape), dt))

    def _sem(self, key):
        if key[0] == 'd':
            return self.dsem[key[1]][key[2]]
        e, ep = key
        while len(self.csem[e]) <= ep:
            self.csem[e].append(self.st.enter_context(self.nc.semaphore(f"c_{e}{len(self.csem[e])}")))
        return self.csem[e][ep]

    def _wait(self, eng, tok):
        key, val = tok
        if key[0] != 'd':
            e, ep = key
            if e == 'pe' and eng == 'pe':
                return
            sk = (eng, e)
            cur = self.seen.get(sk, (-1, 0))
            if (ep, val) <= cur:
                return
            self.seen[sk] = (ep, val)
        else:
            sk = (eng, key)
            if self.seen.get(sk, 0) >= val:
                return
            self.seen[sk] = val
        self.eng[eng].wait_ge(self._sem(key), val)
        self.nwaits += 1

    def _deps(self, eng, r, w):
        toks = []
        for x in r:
            t = self.lastw.get(x)
            if t is not None:
                toks.append(t)
        for x in w:
            t = self.lastw.get(x)
            if t is not None:
                toks.append(t)
            toks.extend(self.reads.get(x, ()))
        for t in toks:
            self._wait(eng, t)

    def _record(self, tok, r, w):
        for x in r:
            self.reads.setdefault(x, []).append(tok)
        for x in w:
            self.lastw[x] = tok
            self.reads[x] = []

    @staticmethod
    def _excl(r, w):
        r2 = [x for x in r if not (isinstance(x, str) and x.startswith('ps'))]
        w2 = list(w) + [x for x in r if isinstance(x, str) and x.startswith('ps')]
        return r2, w2

    def _emit(self, eng, fn):
        ins = fn(self.eng[eng])
        ep, c = divmod(self.cnt[eng], self.EPOCH)
        key = (eng, ep)
        ins.then_inc(self._sem(key), 1)
        self.cnt[eng] += 1
        self.ninst += 1
        return (key, c + 1)

    def op(self, eng, fn, r=(), w=()):
        psr = [x for x in r if isinstance(x, str) and x.startswith('ps')]
        r, w = self._excl(r, w)
        self._deps(eng, r, w)
        if psr and eng in ('act', 'dve') and self.scratch is not None:
            sc0 = self.scratch[eng]
            self._deps(eng, [], ['scr_' + eng])
            tok0 = self._emit(eng, (lambda e: e.memzero(sc0)) if eng == 'act' else (lambda e: e.memset(sc0, 0.0)))
            self._record(tok0, [], ['scr_' + eng])
        tok = self._emit(eng, fn)
        if psr and eng in ('act', 'dve') and self.scratch is not None:
            self._record(tok, r, [x for x in w if x not in psr])
            sc = self.scratch[eng]
            self._deps(eng, [], ['scr_' + eng])
            tok2 = self._emit(eng, (lambda e: e.memzero(sc)) if eng == 'act' else (lambda e: e.memset(sc, 0.0)))
            self._record(tok2, [], psr + ['scr_' + eng])
            return tok
        self._record(tok, r, w)
        return tok

    def dma(self, q, out, in_, r=(), w=(), **kw):
        i = self.drr[q]
        self.drr[q] = (i + 1) % self.NDS
        key = ('d', q, i)
        if self.dval[q][i] > 0:
            self._wait(q, (key, self.dval[q][i]))
        self._deps(q, r, w)
        self.eng[q].dma_start(out=out, in_=in_, **kw).then_inc(self.dsem[q][i], 16)
        self.dval[q][i] += 16
        self.ninst += 1
        tok = (key, self.dval[q][i])
        self._record(tok, r, w)
        return tok

    def barrier(self):
        toks = []
        for q in self.dsem:
            for i in range(self.NDS):
                if self.dval[q][i] > 0:
                    toks.append((('d', q, i), self.dval[q][i]))
        for e in self.csem:
            if self.cnt[e] > 0:
                ep, c = divmod(self.cnt[e] - 1, self.EPOCH)
                toks.append(((e, ep), c + 1))
        for eng in ('pe', 'act', 'dve', 'pool', 'sp'):
            for t in toks:
                if t[0][0] == eng and eng != 'pe':
                    pass
                self._wait(eng, t)

    def finish(self):
        for q in self.dsem:
            for i in range(self.NDS):
                if self.dval[q][i] > 0:
                    self._wait('sp', (('d', q, i), self.dval[q][i]))
        for e in self.csem:
            if self.cnt[e] > 0:
                ep, c = divmod(self.cnt[e] - 1, self.EPOCH)
                self._wait('sp', ((e, ep), c + 1))


T = 2048
NT = 16
NB = 4
DM = 1024
EPS = 1e-6
NEGB = -30000.0
OFF = dict(r_q=0, r_k=256, r_v=512, r_g=1024, g_q=1536, g_k=2048, g_v=2560, g_a=3072, g_b=3076,
           g_z=3080, d_q=3592, d_k=4360, d_v=5128, gates=5896)
GAM = [1.0 - 2.0 ** (-5 - h) for h in range(4)]


def _const_layout():
    cols = {}
    o = 0
    for name, n in (('ident', 128), ('ones', 128), ('triT', 128), ('neg_su', 128), ('neg_ui', 128),
                    ('lvl', 7 * 128), ('m_own4', 512), ('m_prev4', 512), ('decT', 512), ('qdec', 256),
                    ('kdecT', 256), ('perm_ret', 128), ('perm_dil', 128), ('invf', 2)):
        cols[name] = (o, n)
        o += n
    return cols, o


CL, NCST = _const_layout()


def make_consts():
    c = np.zeros((128, NCST), np.float64)
    ar = np.arange(128)
    I, J = ar[:, None], ar[None, :]

    def put(name, a):
        o, n = CL[name]
        c[:, o:o + n] = a
    put('ident', (I == J) * 1.0)
    put('ones', np.ones((128, 128)))
    put('triT', (I <= J) * 1.0)
    put('neg_su', np.where(J > I, 0.0, NEGB))
    put('neg_ui', np.where(J >= I, 0.0, NEGB))
    lv = []
    for lb in range(7):
        b = 1 << lb
        lv.append(((I // (2 * b) == J // (2 * b)) & (I % (2 * b) >= b) & (J % (2 * b) < b)) * 1.0)
    put('lvl', np.concatenate(lv, axis=1))
    put('m_own4', np.tile((J >= I) * 1.0, (1, 4)))
    put('m_prev4', np.tile((J <= I) * 1.0, (1, 4)))
    lg = [np.log1p(-2.0 ** (-5 - h)) for h in range(4)]
    put('decT', np.concatenate([np.where(J >= I, np.exp(np.maximum(J - I, 0) * lg[h]), 0.0) for h in range(4)], axis=1))
    qd = np.zeros((128, 256))
    for j in range(2):
        for p in range(128):
            qd[p, j * 128:(j + 1) * 128] = np.exp((ar + 1.0) * lg[2 * j + p // 64])
    put('qdec', qd)
    kd = np.zeros((128, 256))
    for h in range(4):
        kd[:, h * 64:(h + 1) * 64] = np.exp((127.0 - ar) * lg[h])[:, None]
    put('kdecT', kd)
    pr = np.zeros((128, 128))
    pdl = np.zeros((128, 128))
    for m in range(128):
        d = m % 64
        if d < 32:
            pr[m + 32, m] = -1.0
        else:
            pr[m - 32, m] = 1.0
        if d < 8:
            pdl[m + 8, m] = -1.0
        elif d < 16:
            pdl[m - 8, m] = 1.0
    put('perm_ret', pr)
    put('perm_dil', pdl)
    ret_inv = 1.0 / (10000.0 ** np.linspace(0.0, 1.0, 32, dtype=np.float32).astype(np.float64))
    dil_inv = 500000.0 ** (-np.arange(0, 16, 2, dtype=np.float32).astype(np.float64) / 16)
    iv = np.zeros((128, 2))
    for m in range(128):
        d = m % 64
        iv[m, 0] = np.float32(ret_inv[d % 32])
        iv[m, 1] = np.float32(dil_inv[d % 8]) if d < 16 else 0.0
    put('invf', iv)
    return c.astype(np.float32)


PRM_L = 32 + 4 + 1 + 48 + 64 + 64 + 132 + 44
PO = dict(g_mix=0, g_xat=8, g_mem=16, g_ffn=24, retg=32, gdng=36, gconv=37, alog=85, dtb=149, fconv=213, fbias=345)
NPRM = 2 * PRM_L + 8


def make_prm(inp):
    p = np.zeros((128, NPRM), np.float32)

    def fm(v):
        return np.ascontiguousarray(v.reshape(8, 128).T)
    for l in range(2):
        b = l * PRM_L
        p[:, b + 0:b + 8] = fm(inp['norm_mix_g'][l])
        p[:, b + 8:b + 16] = fm(inp['norm_xattn_g'][l])
        p[:, b + 16:b + 24] = fm(inp['norm_mem_g'][l])
        p[:, b + 24:b + 32] = fm(inp['norm_ffn_g'][l])
        p[:, b + 32:b + 36] = inp['ret_norm_g'][l].T
        p[:, b + 36] = inp['gdn_norm_g'][l]
        p[:, b + 37:b + 85] = inp['gdn_conv_w'][l].reshape(4, 12, 128).transpose(2, 1, 0).reshape(128, 48)
        p[:, b + 85:b + 149] = np.tile(inp['gdn_a_log'][l][None, :], (128, 16))
        p[:, b + 149:b + 213] = np.tile(inp['gdn_dt_bias'][l][None, :], (128, 16))
        p[:, b + 213:b + 345] = inp['ffn_conv_w'][l].reshape(3, 44, 128).transpose(2, 1, 0).reshape(128, 132)
        p[:, b + 345:b + 389] = inp['ffn_conv_b'][l].reshape(44, 128).T
    p[:, 2 * PRM_L:2 * PRM_L + 8] = fm(inp['final_norm_g'])
    return p
def build(nseq=2, nlayer=2, dbg=False, phases='rgdoxf'):
    nc = bass.Bass("TRN2", target_bir_lowering=False)
    D = {}
    def din(name, shape, dt=F32):
        D[name] = nc.dram_tensor(name, list(shape), dt, kind="ExternalInput").ap()
        return D[name]
    x_d = din('x', [nseq, T, DM])
    mem_d = din('mem', [nseq, 256, DM])
    pos_d = din('positions', [nseq, T], I32)
    cst_d = din('consts', [128, NCST])
    prm_d = din('prm', [128, NPRM])
    w_in_d = din('w_in', [2, DM, 8968])
    wbr_d = [din('w_br_ret', [2, 512, DM]), din('w_br_gdn', [2, 512, DM]), din('w_br_dil', [2, 256, DM])]
    w_out_d = din('w_out', [2, DM, DM])
    wq_d = din('xattn_wq', [2, DM, 512])
    wkv_d = din('xattn_wkv', [2, DM, 1024])
    wo_d = din('xattn_wo', [2, 512, DM])
    wup_d = din('ffn_w_up', [2, DM, 5632])
    wdn_d = din('ffn_w_down', [2, 2816, DM])
    out_d = nc.dram_tensor('out', [nseq, T, DM], F32, kind="ExternalOutput").ap()
    xsp_d = nc.dram_tensor('xspill', [128, 8 * T], F32, kind="ExternalOutput").ap()
    tab_d = nc.dram_tensor('tabs', [4, 128, T], F32, kind="ExternalOutput").ap()
    dbg_d = nc.dram_tensor('dbg', [128, 8 * T], F32, kind="ExternalOutput").ap() if dbg else None

    def kp(ap2d):
        return ap2d.rearrange("(k p) n -> p k n", p=128)

    with ExitStack() as st:
        S = Sched(nc, st)
        ARW = 53000
        arena = S.sb("arena", [128, ARW], F32)
        PS = [S.ps(f"ps{i}", [128, 512], F32) for i in range(8)]
        S.scratch = {'act': S.sb('scr_act', [128, 2], F32)[:], 'dve': S.sb('scr_dve', [128, 2], F32)[:]}
        PSK = [f"ps{i}" for i in range(8)]
        astate = {'top': 0, 'gen': 0}

        def alloc(name, shape, dt=F32, reg='m'):
            n = 1
            for s_ in shape[1:]:
                n *= s_
            words = n if dt != BF16 else (n + 1) // 2
            if reg == 'x':
                o = astate['xtop']
                assert o + words <= astate['xend'], (name, o, words)
                astate['xtop'] = o + words
            else:
                o = astate['top']
                assert o + words <= ARW, (name, o, words)
                astate['top'] = o + words
            ap = arena[0:shape[0], o:o + words]
            if dt != F32:
                ap = ap.bitcast(dt)
            if dt == BF16 and n % 2:
                ap = ap[:, 0:n]
            if len(shape) == 3:
                ap = ap.rearrange("p (a b) -> p a b", a=shape[1])
            elif len(shape) == 4:
                ap = ap.rearrange("p (a b c) -> p a b c", a=shape[1], b=shape[2])
            return ap

        def run_gens(gens):
            gens = list(gens)
            while gens:
                for g_ in list(gens):
                    try:
                        next(g_)
                    except StopIteration:
                        gens.remove(g_)

        def mark():
            return (astate['top'], astate.get('xtop', 0))

        def release(m):
            astate['top'] = m[0]
            astate['xtop'] = m[1]
            astate['gen'] += 1
            S.barrier()

        def K(name, *idx):
            return (name, astate['gen']) + idx

        def mm(out, lhsT, rhs, start, stop, r, w):
            S.op('pe', lambda e: e.matmul(out, lhsT, rhs, start=start, stop=stop), r=r, w=w)

        def tr(out, in_, ident, r, w):
            S.op('pe', lambda e: e.transpose(out, in_, ident), r=r, w=w)

        def act(out, in_, func, r, w, **kw):
            S.op('act', lambda e: e.activation(out, in_, func, **kw), r=r, w=w)

        def tt(eng, out, a, b, op, r, w):
            S.op(eng, lambda e: e.tensor_tensor(out, a, b, op), r=r, w=w)

        def ts(eng, out, a, s1, s2, op0, op1, r, w):
            if op1 is None:
                S.op(eng, lambda e: e.tensor_scalar(out, a, s1, None, op0), r=r, w=w)
            else:
                S.op(eng, lambda e: e.tensor_scalar(out, a, s1, s2, op0, op1), r=r, w=w)

        def stt(eng, out, a, sc, b, op0, op1, r, w):
            S.op(eng, lambda e: e.scalar_tensor_tensor(out, a, sc, b, op0, op1), r=r, w=w)

        def cp(eng, out, in_, r, w):
            if eng == 'act':
                S.op('act', lambda e: e.copy(out, in_), r=r, w=w)
            else:
                S.op(eng, lambda e: e.tensor_copy(out, in_), r=r, w=w)

        def memset(eng, ap, val, w):
            S.op(eng, lambda e: e.memset(ap, val), r=[], w=w)

        def recip(out, in_, r, w):
            S.op('dve', lambda e: e.reciprocal(out, in_), r=r, w=w)

        cst = alloc('cst', [128, NCST])
        prm = alloc('prm', [128, NPRM])
        identb = alloc('identb', [128, 128], BF16)
        onesb = alloc('onesb', [128, 128], BF16)
        astate['xtop'] = astate['top']
        xT = alloc('xT', [128, 8, T])
        astate['xend'] = astate['top']
        xnT = alloc('xnT', [128, 8, T], BF16)
        memh = alloc('memh', [128, 8, 256])
        S.dma('sp', cst, cst_d, r=[], w=['cst'])
        S.dma('sp', prm, prm_d, r=[], w=['prm'])

        def C(name, a=None, b=None):
            o, n = CL[name]
            if a is None:
                return cst[:, o:o + n]
            return cst[:, o + a:o + b]
        cp('dve', identb, C('ident'), r=['cst'], w=['identb'])
        cp('dve', onesb, C('ones'), r=['cst'], w=['onesb'])
        negm = [alloc('negown', [128, 128], BF16), alloc('negprev', [128, 128], BF16)]
        negsu_b = alloc('negsu_b', [128, 128], BF16)
        negui_b = alloc('negui_b', [128, 128], BF16)
        permr_b = alloc('permr_b', [128, 128], BF16)
        permd_b = alloc('permd_b', [128, 128], BF16)
        cp('dve', negsu_b, C('neg_su'), r=['cst'], w=['negm'])
        cp('dve', negui_b, C('neg_ui'), r=['cst'], w=['negm'])
        cp('dve', permr_b, C('perm_ret'), r=['cst'], w=['negm'])
        cp('dve', permd_b, C('perm_dil'), r=['cst'], w=['negm'])
        ts('dve', negm[0], C('m_own4', 0, 128), 1.0, 30000.0, ALU.subtract, ALU.mult, r=['cst'], w=['negm'])
        ts('dve', negm[1], C('m_prev4', 0, 128), 1.0, 30000.0, ALU.subtract, ALU.mult, r=['cst'], w=['negm'])
        identf = C('ident')
        onesf = C('ones')
        base_mark = mark()

        def wload(dst, src, key):
            S.dma('pool', dst, src, r=[], w=[key])

        def PR(l, name, a, b):
            o = l * PRM_L + PO[name]
            return prm[:, o + a:o + b]

        def rmsnorm_xn(gcols):
            m = mark()
            sq = alloc('sq', [128, 2, 512], BF16)
            rs = alloc('rs', [128, 512])
            for tb in range(NB):
                sl = slice(tb * 512, (tb + 1) * 512)
                for c in range(8):
                    act(sq[:, c % 2, :], xT[:, c, sl], AF.Square, r=[('xT', c, tb)], w=[K('sq', c % 2)])
                    mm(PS[0][:, :], onesb, sq[:, c % 2, :], c == 0, c == 7, r=[K('sq', c % 2), 'onesb'], w=[PSK[0]])
                act(rs, PS[0][:, :], AF.Sqrt, r=[PSK[0]], w=[K('rs')], bias=EPS, scale=1.0 / DM)
                recip(rs, rs, r=[K('rs')], w=[K('rs')])
                for c in range(8):
                    stt('dve', xnT[:, c, sl], xT[:, c, sl], gcols[:, c:c + 1], rs, ALU.mult, ALU.mult,
                        r=[('xT', c, tb), K('rs'), 'prm'], w=[('xnT', tb)])
            release(m)

        def projF(ps, psk, w, wk, cols, rhs_fn, rkeys):
            for k in range(8):
                mm(ps, w[:, k, cols], rhs_fn(k), k == 0, k == 7, r=[wk] + rkeys, w=[psk])

        def xn_blk(tb):
            return lambda k: xnT[:, k, tb * 512:(tb + 1) * 512]

        def branch_proj(l, b, oT, okey, nk, first):
            m = mark()
            wb = [alloc(f'wb{i}', [128, nk, 128], BF16) for i in range(2)]
            wg = [alloc(f'wg{i}', [128, 8, 128], BF16) for i in range(2)]
            G = alloc('G', [128, 512])
            tmp = alloc('bp_tmp', [128, 512], BF16)
            wbr_v = kp(wbr_d[b][l])
            win_v = kp(w_in_d[l])
            for c in range(8):
                i = c % 2
                wload(wb[i], wbr_v[:, :, c * 128:(c + 1) * 128], K('wb', i))
                wload(wg[i], win_v[:, :, OFF['gates'] + b * 1024 + c * 128:OFF['gates'] + b * 1024 + (c + 1) * 128], K('wg', i))
                for tb in range(NB):
                    sl = slice(tb * 512, (tb + 1) * 512)
                    for k in range(nk):
                        mm(PS[0][:, :], wb[i][:, k, :], oT[:, k, sl], k == 0, k == nk - 1, r=[K('wb', i), okey], w=[PSK[0]])
                    projF(PS[1][:, :], PSK[1], wg[i], K('wg', i), slice(0, 128), xn_blk(tb), [('xnT', tb)])
                    act(G, PS[1][:, :], AF.Sigmoid, r=[PSK[1]], w=[K('G')])
                    if first:
                        tt('dve', H['merged'][:, c, sl], PS[0][:, :], G, ALU.mult, r=[PSK[0], K('G')], w=[('merged', c, tb)])
                    else:
                        tt('dve', tmp, PS[0][:, :], G, ALU.mult, r=[PSK[0], K('G')], w=[K('bp_tmp')])
                        tt('dve', H['merged'][:, c, sl], H['merged'][:, c, sl], tmp, ALU.add, r=[K('bp_tmp'), ('merged', c, tb)], w=[('merged', c, tb)])
            release(m)

        def head_norm_gate(pso, psok, wg, wgk, gcols, tb, gcol, dst, dkey, tmpk):
            sq, rs, sg, t1 = tmpk
            act(sq, pso, AF.Square, r=[psok], w=[K('hn_sq')])
            mm(PS[6][:, :], onesb, sq, True, True, r=[K('hn_sq'), 'onesb'], w=[PSK[6]])
            act(rs, PS[6][:, :], AF.Sqrt, r=[PSK[6]], w=[K('hn_rs')], bias=EPS, scale=1.0 / 128)
            recip(rs, rs, r=[K('hn_rs')], w=[K('hn_rs')])
            projF(PS[7][:, :], PSK[7], wg, wgk, gcols, xn_blk(tb), [('xnT', tb)])
            act(sg, PS[7][:, :], AF.Silu, r=[PSK[7]], w=[K('hn_sg')])
            tt('dve', t1, pso, rs, ALU.mult, r=[psok, K('hn_rs')], w=[K('hn_t1')])
            stt('dve', dst, t1, gcol, sg, ALU.mult, ALU.mult, r=[K('hn_t1'), K('hn_sg'), 'prm'], w=[dkey])

        def hn_tmps():
            return (alloc('hn_sq', [128, 512], BF16), alloc('hn_rs', [128, 512]), alloc('hn_sg', [128, 512]),
                    alloc('hn_t1', [128, 512]))

        def rotary_block(ps_src, psk_src, perm, tabC, tabS, tkeys, scale, dst, dkey, tmps):
            xq, t1, t2 = tmps
            if scale == 1.0:
                cp('act', xq, ps_src, r=[psk_src], w=[K('ro_xq')])
            else:
                S.op('act', lambda e: e.mul(xq, ps_src, float(scale)), r=[psk_src], w=[K('ro_xq')])
            xqb = t2.bitcast(BF16)[:, 0:512]
            cp('act', xqb, xq, r=[K('ro_xq')], w=[K('ro_t2')])
            mm(PS[5][:, :], perm, xqb, True, True, r=[K('ro_t2'), 'negm'], w=[PSK[5]])
            tt('dve', t1, PS[5][:, :], tabS, ALU.mult, r=[PSK[5]] + tkeys, w=[K('ro_t1')])
            tt('dve', t2, xq, tabC, ALU.mult, r=[K('ro_xq')] + tkeys, w=[K('ro_t2')])
            tt('dve', dst, t1, t2, ALU.add, r=[K('ro_t1'), K('ro_t2')], w=[dkey])

        def ret_branch(l):
            m0 = mark()
            oT = alloc('ret_oT', [128, 4, T], BF16, reg='x')
            win_v = kp(w_in_d[l])
            for j in range(2):
                m1 = mark()
                wq = alloc('rwq', [128, 8, 128], BF16, reg='x')
                wk = alloc('rwk', [128, 8, 128], BF16, reg='x')
                wv = alloc('rwv', [128, 8, 256], BF16, reg='x')
                wg = alloc('rwg', [128, 8, 256], BF16, reg='x')
                wload(wq, win_v[:, :, OFF['r_q'] + j * 128:OFF['r_q'] + (j + 1) * 128], K('rwq'))
                wload(wk, win_v[:, :, OFF['r_k'] + j * 128:OFF['r_k'] + (j + 1) * 128], K('rwk'))
                wload(wv, win_v[:, :, OFF['r_v'] + j * 256:OFF['r_v'] + (j + 1) * 256], K('rwv'))
                wload(wg, win_v[:, :, OFF['r_g'] + j * 256:OFF['r_g'] + (j + 1) * 256], K('rwg'))
                qT = alloc('rqT', [128, T], BF16, reg='x')
                kT = alloc('rkT', [128, T], BF16, reg='x')
                qdT = alloc('rqdT', [128, T], BF16, reg='x')
                vtok = alloc('rv', [128, NT, 256], BF16, reg='x')
                kd = alloc('rkd', [128, NT, 128], BF16, reg='x')
                tC = alloc('rtC', [128, 512])
                tS = alloc('rtS', [128, 512])
                rot_t = (alloc('ro_xq', [128, 512]), alloc('ro_t1', [128, 512]), alloc('ro_t2', [128, 512]))
                for tb in range(NB):
                    sl = slice(tb * 512, (tb + 1) * 512)
                    S.dma('sp', tC, tab_d[0, :, sl], r=['tabs'], w=[K('rtC')])
                    S.dma('sp', tS, tab_d[1, :, sl], r=['tabs'], w=[K('rtS')])
                    for which in range(2):
                        w_, wk_ = (wq, K('rwq')) if which == 0 else (wk, K('rwk'))
                        projF(PS[0][:, :], PSK[0], w_, wk_, slice(0, 128), xn_blk(tb), [('xnT', tb)])
                        dst = qT if which == 0 else kT
                        rotary_block(PS[0][:, :], PSK[0], permr_b, tC, tS, [K('rtC'), K('rtS')],
                                     1.0 if which == 0 else 0.125, dst[:, sl], K('rq' if which == 0 else 'rk', tb), rot_t)
                    for n in range(tb * 4, tb * 4 + 4):
                        nsl = slice(n * 128, (n + 1) * 128)
                        tt('dve', qdT[:, nsl], qT[:, nsl], C('qdec', j * 128, (j + 1) * 128), ALU.mult,
                           r=[K('rq', tb), 'cst'], w=[K('rqd', n)])
                        for k in range(8):
                            mm(PS[1][:, 0:256], xnT[:, k, nsl], wv[:, k, :], k == 0, k == 7, r=[('xnT', tb), K('rwv')], w=[PSK[1]])
                        cp('act', vtok[:, n, :], PS[1][:, 0:256], r=[PSK[1]], w=[K('rv', n)])
                        pb = PS[2][:, 0:64].bitcast(BF16)
                        tr(pb, kT[:, nsl], identb, r=[K('rk', tb), 'identb'], w=[PSK[2]])
                        tt('dve', kd[:, n, :], pb, C('kdecT', j * 128, (j + 1) * 128), ALU.mult, r=[PSK[2], 'cst'], w=[K('rkd', n)])
                st_f = alloc('rst_f', [128, 128])
                st_b = alloc('rst_b', [128, 128], BF16)
                STs = alloc('rSTs', [128, 2, 128], BF16)
                hnt = hn_tmps()
                memset('dve', st_f, 0.0, [K('rst_f')])
                memset('pool', st_b, 0.0, [K('rst_b')])
                for tb in range(NB):
                    sl = slice(tb * 512, (tb + 1) * 512)
                    for nl in range(4):
                        n = tb * 4 + nl
                        nsl = slice(n * 128, (n + 1) * 128)
                        osl = slice(nl * 128, (nl + 1) * 128)
                        for a in range(2):
                            h = 2 * j + a
                            rows = slice(a * 64, (a + 1) * 64)
                            bk = 2 if a == 0 else 5
                            mm(PS[bk][:, 0:128], kT[rows, nsl], qT[rows, nsl], True, True,
                               r=[K('rk', tb), K('rq', tb)], w=[PSK[bk]])
                        for a in range(2):
                            bk = 2 if a == 0 else 5
                            tt('dve', STs[:, a, :], PS[bk][:, 0:128],
                               C('decT', (2 * j + a) * 128, (2 * j + a + 1) * 128), ALU.mult, r=[PSK[bk], 'cst'], w=[K('rSTs')])
                        for a in range(2):
                            rows = slice(a * 64, (a + 1) * 64)
                            mm(PS[3 + a][:, osl], vtok[:, n, a * 128:(a + 1) * 128], STs[:, a, :], True, False,
                               r=[K('rv', n), K('rSTs')], w=[PSK[3 + a]])
                            mm(PS[3 + a][:, osl], st_b[rows, :], qdT[rows, nsl], False, True,
                               r=[K('rst_b'), K('rqd', n)], w=[PSK[3 + a]])
                        mm(PS[1][:, 0:256], kd[:, n, :], vtok[:, n, :], True, True, r=[K('rkd', n), K('rv', n)], w=[PSK[1]])
                        for a in range(2):
                            rows = slice(a * 64, (a + 1) * 64)
                            stt('dve', st_f[rows, :], st_f[rows, :], float(GAM[2 * j + a] ** 128), PS[1][rows, a * 128:(a + 1) * 128],
                                ALU.mult, ALU.add, r=[K('rst_f'), PSK[1]], w=[K('rst_f')])
                        cp('act', st_b, st_f, r=[K('rst_f')], w=[K('rst_b')])
                    for a in range(2):
                        h = 2 * j + a
                        head_norm_gate(PS[3 + a][:, :], PSK[3 + a], wg, K('rwg'), slice(a * 128, (a + 1) * 128), tb,
                                       PR(l, 'retg', h, h + 1), oT[:, h, sl], K('ret_oT', tb), hnt)
                release(m1)
            branch_proj(l, 0, oT, None, 4, True)
            release(m0)
        def gdn_branch(l):
            m0 = mark()
            oT = alloc('gdn_oT', [128, 4, T], BF16, reg='x')
            win_v = kp(w_in_d[l])
            wab = alloc('gwab', [128, 8, 8], BF16)
            wload(wab, win_v[:, :, OFF['g_a']:OFF['g_a'] + 8], K('gwab'))
            AB = alloc('gAB', [128, NT, 8])
            for n in range(NT):
                for k in range(8):
                    mm(PS[0][:, n * 8:(n + 1) * 8], xnT[:, k, n * 128:(n + 1) * 128], wab[:, k, :], k == 0, k == 7,
                       r=[('xnT', n // 4), K('gwab')], w=[PSK[0]])
            cp('act', AB.rearrange("p a b -> p (a b)"), PS[0][:, 0:128], r=[PSK[0]], w=[K('gAB')])
            names = ['g', 'lnb', 'gc', 'gtot', 'rr', 'bg', 'beta', 'kdecs', 'egl', 'tmpa', 'negA', 'ngc', 'rr_h', 'rr_l', 'gc_h', 'gc_l', 'ngc_h', 'ngc_l']
            sc = {nm: alloc('gs_' + nm, [128, NT, 4]) for nm in names}
            f2 = lambda a: a.rearrange("p a b -> p (a b)")
            ka = K('gsc')
            act(f2(sc['negA']), PR(l, 'alog', 0, 64), AF.Exp, r=['prm'], w=[ka])
            tt('dve', sc['tmpa'], AB[:, :, 0:4], PR(l, 'dtb', 0, 64).rearrange("p (a b) -> p a b", b=4), ALU.add, r=[K('gAB'), 'prm', ka], w=[ka])
            act(f2(sc['tmpa']), f2(sc['tmpa']), AF.Exp, r=[ka], w=[ka])
            act(f2(sc['tmpa']), f2(sc['tmpa']), AF.Ln, r=[ka], w=[ka], bias=1.0)
            stt('dve', f2(sc['g']), f2(sc['tmpa']), -1.0, f2(sc['negA']), ALU.mult, ALU.mult, r=[ka], w=[ka])
            act(sc['lnb'], AB[:, :, 4:8], AF.Exp, r=[K('gAB'), ka], w=[ka], scale=-1.0)
            act(f2(sc['lnb']), f2(sc['lnb']), AF.Ln, r=[ka], w=[ka], bias=1.0)
            ts('dve', f2(sc['lnb']), f2(sc['lnb']), -1.0, None, ALU.mult, None, r=[ka], w=[ka])
            mm(PS[1][:, 0:64], C('triT'), f2(sc['g']), True, True, r=[ka, 'cst'], w=[PSK[1]])
            mm(PS[1][:, 64:128], onesf, f2(sc['g']), True, True, r=[ka, 'cst'], w=[PSK[1]])
            cp('act', f2(sc['gc']), PS[1][:, 0:64], r=[PSK[1]], w=[ka])
            cp('act', f2(sc['gtot']), PS[1][:, 64:128], r=[PSK[1]], w=[ka])
            tt('dve', f2(sc['rr']), f2(sc['gc']), f2(sc['lnb']), ALU.add, r=[ka], w=[ka])
            ts('dve', f2(sc['ngc']), f2(sc['gc']), -1.0, None, ALU.mult, None, r=[ka], w=[ka])
            act(f2(sc['bg']), f2(sc['rr']), AF.Exp, r=[ka], w=[ka])
            act(f2(sc['beta']), f2(sc['lnb']), AF.Exp, r=[ka], w=[ka])
            tt('dve', f2(sc['kdecs']), f2(sc['gtot']), f2(sc['gc']), ALU.subtract, r=[ka], w=[ka])
            act(f2(sc['kdecs']), f2(sc['kdecs']), AF.Exp, r=[ka], w=[ka])
            act(f2(sc['egl']), f2(sc['gtot']), AF.Exp, r=[ka], w=[ka])
            hb = alloc('gs_hb', [128, 64], BF16)
            for nm_ in ('rr', 'gc'):
                cp('dve', hb, f2(sc[nm_]), r=[ka], w=[ka])
                cp('dve', f2(sc[nm_ + '_h']), hb, r=[ka], w=[ka])
                tt('dve', f2(sc[nm_ + '_l']), f2(sc[nm_]), f2(sc[nm_ + '_h']), ALU.subtract, r=[ka], w=[ka])
                cp('dve', hb, f2(sc[nm_ + '_l']), r=[ka], w=[ka])
                cp('dve', f2(sc[nm_ + '_l']), hb, r=[ka], w=[ka])
            ts('dve', f2(sc['ngc_h']), f2(sc['gc_h']), -1.0, None, ALU.mult, None, r=[ka], w=[ka])
            ts('dve', f2(sc['ngc_l']), f2(sc['gc_l']), -1.0, None, ALU.mult, None, r=[ka], w=[ka])
            S.barrier()
            for h in range(4):
                m1 = mark()
                wq = alloc('gwq', [128, 8, 128], BF16)
                wk = alloc('gwk', [128, 8, 128], BF16)
                wv = alloc('gwv', [128, 8, 128], BF16)
                wz = alloc('gwz', [128, 8, 128], BF16)
                for w_, nm in ((wq, 'g_q'), (wk, 'g_k'), (wv, 'g_v'), (wz, 'g_z')):
                    wload(w_, win_v[:, :, OFF[nm] + h * 128:OFF[nm] + (h + 1) * 128], K('gw' + nm))
                qT = alloc('gqT', [128, T], BF16, reg='x')
                kT = alloc('gkT', [128, T], BF16, reg='x')
                qgT = alloc('gqgT', [128, T], BF16, reg='x')
                ktok = alloc('gktok', [128, NT, 128], BF16, reg='x')
                vtok = alloc('gvtok', [128, NT, 128], BF16, reg='x')
                kdtok = alloc('gkdtok', [128, NT, 128], BF16, reg='x')
                U = alloc('gU', [128, NT, 128], reg='x')
                WT = alloc('gWT', [128, NT, 128], BF16, reg='x')
                attnT = alloc('gattnT', [128, NT, 128], BF16, reg='x')
                mconv = mark()
                wlist = ((wq, 'g_q'), (wk, 'g_k'), (wv, 'g_v'))
                banksets = ((0, 1, 2, None), (3, 4, 5, 4), (6, 7, None, 7))
                dws = []
                for which in range(3):
                    row = []
                    for jj in range(4):
                        dw_ = alloc(f'gdw{which}{jj}', [128, 128], BF16)
                        wc_ = PR(l, 'gconv', (which * 4 + h) * 4 + jj, (which * 4 + h) * 4 + jj + 1)
                        S.op('act', (lambda e, dw_=dw_, wc_=wc_: e.mul(dw_, identb, wc_)), r=['identb', 'prm'], w=[K('gdw')])
                        row.append(dw_)
                    dws.append(row)

                def conv_gen(which):
                    w_, nm = wlist[which]
                    bp, bc_, bo, bt = banksets[which]
                    raw = [alloc(f'graw{which}{i}', [128, 516], BF16) for i in range(2)]
                    cc = alloc(f'gcc{which}', [128, 512])
                    sq = alloc(f'gsq{which}', [128, 512], BF16)
                    rs = alloc(f'grs{which}', [128, 512]) if which < 2 else None
                    yield
                    for tb in range(NB):
                        sl = slice(tb * 512, (tb + 1) * 512)
                        rw = raw[tb % 2]
                        rk = K('graw', which, tb % 2)
                        projF(PS[bp][:, :], PSK[bp], w_, K('gw' + nm), slice(0, 128), xn_blk(tb), [('xnT', tb)])
                        yield
                        if tb == 0:
                            memset('dve', rw[:, 0:3], 0.0, [rk])
                        else:
                            cp('act', rw[:, 0:3], raw[(tb - 1) % 2][:, 512:515], r=[K('graw', which, (tb - 1) % 2)], w=[rk])
                        cp('act', rw[:, 3:515], PS[bp][:, :], r=[PSK[bp]], w=[rk])
                        yield
                        for jj in range(4):
                            mm(PS[bc_][:, :], dws[which][jj], rw[:, jj:jj + 512], jj == 0, jj == 3, r=[rk, K('gdw')], w=[PSK[bc_]])
                        yield
                        act(cc, PS[bc_][:, :], AF.Silu, r=[PSK[bc_]], w=[K('gcc', which)])
                        yield
                        if which < 2:
                            act(sq, cc, AF.Square, r=[K('gcc', which)], w=[K('gsq', which)])
                            yield
                            mm(PS[bo][:, :], onesb, sq, True, True, r=[K('gsq', which), 'onesb'], w=[PSK[bo]])
                            yield
                            act(rs, PS[bo][:, :], AF.Sqrt, r=[PSK[bo]], w=[K('grs', which)], bias=EPS, scale=1.0)
                            yield
                            recip(rs, rs, r=[K('grs', which)], w=[K('grs', which)])
                            dst = qT if which == 0 else kT
                            dk_ = K('gq' if which == 0 else 'gk', tb)
                            stt('dve', dst[:, sl], cc, float(128 ** -0.5) if which == 0 else 1.0, rs, ALU.mult, ALU.mult,
                                r=[K('gcc', which), K('grs', which)], w=[dk_])
                            src = dst[:, sl]
                            skey = dk_
                            yield
                        else:
                            cp('act', sq, cc, r=[K('gcc', which)], w=[K('gsq', which)])
                            src = sq
                            skey = K('gsq', which)
                            yield
                        if which >= 1:
                            pb = PS[bt][:, 0:256].bitcast(BF16)
                            for i in range(4):
                                tr(pb[:, i * 128:(i + 1) * 128], src[:, i * 128:(i + 1) * 128], identb, r=[skey, 'identb'], w=[PSK[bt]])
                            yield
                            dtile = ktok if which == 1 else vtok
                            cp('act', dtile[:, tb * 4:tb * 4 + 4, :].rearrange("p a b -> p (a b)"), pb, r=[PSK[bt]],
                               w=[K('gktok' if which == 1 else 'gvtok', tb)])
                            yield
                run_gens([conv_gen(0), conv_gen(1), conv_gen(2)])
                release(mconv)
                fl = lambda t_: t_.rearrange("p a b -> p (a b)")
                bc4 = lambda m_: m_.unsqueeze(1).broadcast_to([128, 4, 128])
                v4 = lambda p_: p_.rearrange("p (a b) -> p a b", a=4)
                mgrp = mark()
                gsets = []
                for si in range(2):
                    d_ = {}
                    for nm_ in ('argL', 'argA', 'EG'):
                        d_[nm_] = alloc(f'g{nm_}{si}', [128, 4, 128])
                    for nm_ in ('dgr_h', 'dgr_l', 'dgg_h', 'dgg_l', 'dgn_h', 'dgn_l'):
                        d_[nm_] = alloc(f'g{nm_}{si}', [128, 4, 128], BF16)
                    for nm_ in ('LT', 'B', 'BT', 'X', 'vb', 'kbg'):
                        d_[nm_] = alloc(f'g{nm_}{si}', [128, 4, 128], BF16)
                    gsets.append(d_)

                def grp_gen(gq, si):
                    t = gsets[si]
                    p0, p1, p2, p3 = [4 * si + i_ for i_ in range(4)]
                    kk = lambda nm_: K('gg' + nm_, si)
                    n0 = gq * 4
                    tb = gq
                    gsl = slice(n0 * 128, (n0 + 4) * 128)
                    colb = lambda nm: sc[nm][:, n0:n0 + 4, h:h + 1].broadcast_to([128, 4, 128])
                    c4 = [slice(c * 128, (c + 1) * 128) for c in range(4)]
                    for dn_, cn_ in (('dgr', 'rr'), ('dgg', 'gc'), ('dgn', 'ngc')):
                        for hl in ('_h', '_l'):
                            tt('dve', t[dn_ + hl], bc4(identb), colb(cn_ + hl), ALU.mult, r=['identb', ka], w=[kk(dn_)])
                    yield
                    mm(PS[p0][:, :], onesb, fl(t['dgg_h']), True, False, r=[kk('dgg'), 'onesb'], w=[PSK[p0]])
                    mm(PS[p0][:, :], onesb, fl(t['dgg_l']), False, True, r=[kk('dgg'), 'onesb'], w=[PSK[p0]])
                    for c in range(4):
                        mm(PS[p1][:, c4[c]], onesb, t['dgr_h'][:, c, :], True, False, r=[kk('dgr'), 'onesb'], w=[PSK[p1]])
                        mm(PS[p1][:, c4[c]], onesb, t['dgr_l'][:, c, :], False, False, r=[kk('dgr'), 'onesb'], w=[PSK[p1]])
                        mm(PS[p1][:, c4[c]], t['dgn_h'][:, c, :], onesb, False, False, r=[kk('dgn'), 'onesb'], w=[PSK[p1]])
                        mm(PS[p1][:, c4[c]], t['dgn_l'][:, c, :], onesb, False, False, r=[kk('dgn'), 'onesb'], w=[PSK[p1]])
                        mm(PS[p1][:, c4[c]], identb, negsu_b, False, True, r=['identb', 'negm'], w=[PSK[p1]])
                    for c in range(4):
                        mm(PS[p2][:, c4[c]], onesb, t['dgg_h'][:, c, :], True, False, r=[kk('dgg'), 'onesb'], w=[PSK[p2]])
                        mm(PS[p2][:, c4[c]], onesb, t['dgg_l'][:, c, :], False, False, r=[kk('dgg'), 'onesb'], w=[PSK[p2]])
                        mm(PS[p2][:, c4[c]], t['dgn_h'][:, c, :], onesb, False, False, r=[kk('dgn'), 'onesb'], w=[PSK[p2]])
                        mm(PS[p2][:, c4[c]], t['dgn_l'][:, c, :], onesb, False, False, r=[kk('dgn'), 'onesb'], w=[PSK[p2]])
                        mm(PS[p2][:, c4[c]], identb, negui_b, False, True, r=['identb', 'negm'], w=[PSK[p2]])
                    yield
                    act(fl(t['EG']), PS[p0][:, :], AF.Exp, r=[PSK[p0]], w=[kk('EG')])
                    act(fl(t['argL']), PS[p1][:, :], AF.Exp, r=[PSK[p1]], w=[kk('argL')])
                    act(fl(t['argA']), PS[p2][:, :], AF.Exp, r=[PSK[p2]], w=[kk('argA')])
                    yield
                    tt('dve', qgT[:, gsl], qT[:, gsl], fl(t['EG']), ALU.mult, r=[K('gq', tb), kk('EG')], w=[K('gqg', gq)])
                    for c in range(4):
                        nsl = slice((n0 + c) * 128, (n0 + c + 1) * 128)
                        mm(PS[p3][:, c4[c]], kT[:, nsl], kT[:, nsl], True, True, r=[K('gk', tb)], w=[PSK[p3]])
                    for c in range(4):
                        nsl = slice((n0 + c) * 128, (n0 + c + 1) * 128)
                        mm(PS[p0][:, c4[c]], kT[:, nsl], qT[:, nsl], True, True, r=[K('gk', tb), K('gq', tb)], w=[PSK[p0]])
                    yield
                    tt('dve', t['LT'], v4(PS[p3][:, :]), t['argL'], ALU.mult, r=[PSK[p3], kk('argL')], w=[kk('LT')])
                    tt('dve', attnT[:, n0:n0 + 4, :], v4(PS[p0][:, :]), t['argA'], ALU.mult, r=[PSK[p0], kk('argA')], w=[K('gattnT', gq)])
                    yield
                    for lb in range(7):
                        mk = bc4(C('lvl', lb * 128, (lb + 1) * 128))
                        Bc = (lambda c: identb) if lb == 0 else (lambda c: t['B'][:, c, :])
                        BTc = (lambda c: identb) if lb == 0 else (lambda c: t['BT'][:, c, :])
                        bkeys = ['identb'] if lb == 0 else [kk('B')]
                        btkeys = ['identb'] if lb == 0 else [kk('BT')]
                        for c in range(4):
                            mm(PS[p1][:, c4[c]], t['LT'][:, c, :], Bc(c), True, True, r=[kk('LT')] + bkeys, w=[PSK[p1]])
                        yield
                        stt('dve', t['X'], v4(PS[p1][:, :]), -1.0, mk, ALU.mult, ALU.mult, r=[PSK[p1], 'cst'], w=[kk('X')])
                        yield
                        if lb < 6:
                            for c in range(4):
                                mm(PS[p2][:, c4[c]], identb, Bc(c), True, False, r=['identb'] + bkeys, w=[PSK[p2]])
                                mm(PS[p2][:, c4[c]], BTc(c), t['X'][:, c, :], False, True, r=btkeys + [kk('X')], w=[PSK[p2]])
                        for c in range(4):
                            mm(PS[p3][:, c4[c]], identb, BTc(c), True, False, r=['identb'] + btkeys, w=[PSK[p3]])
                            mm(PS[p3][:, c4[c]], t['X'][:, c, :], BTc(c), False, True, r=[kk('X')] + btkeys, w=[PSK[p3]])
                        yield
                        if lb < 6:
                            cp('act', fl(t['B']), PS[p2][:, :], r=[PSK[p2]], w=[kk('B')])
                        cp('act', fl(t['BT']), PS[p3][:, :], r=[PSK[p3]], w=[kk('BT')])
                        yield
                    tt('dve', t['vb'], vtok[:, n0:n0 + 4, :], colb('beta'), ALU.mult, r=[K('gvtok', tb), ka], w=[kk('vb')])
                    tt('dve', t['kbg'], ktok[:, n0:n0 + 4, :], colb('bg'), ALU.mult, r=[K('gktok', tb), ka], w=[kk('kbg')])
                    tt('dve', kdtok[:, n0:n0 + 4, :], ktok[:, n0:n0 + 4, :], colb('kdecs'), ALU.mult, r=[K('gktok', tb), ka], w=[K('gkdtok', gq)])
                    yield
                    for c in range(4):
                        mm(PS[p0][:, c4[c]], t['BT'][:, c, :], t['vb'][:, c, :], True, True, r=[kk('BT'), kk('vb')], w=[PSK[p0]])
                    for c in range(4):
                        mm(PS[p1][:, c4[c]], t['kbg'][:, c, :], t['BT'][:, c, :], True, True, r=[kk('BT'), kk('kbg')], w=[PSK[p1]])
                    yield
                    cp('act', fl(U[:, n0:n0 + 4, :]), PS[p0][:, :], r=[PSK[p0]], w=[K('gU', gq)])
                    cp('act', fl(WT[:, n0:n0 + 4, :]), PS[p1][:, :], r=[PSK[p1]], w=[K('gWT', gq)])
                    yield
                run_gens([grp_gen(0, 0), grp_gen(1, 1)])
                run_gens([grp_gen(2, 0), grp_gen(3, 1)])
                release(mgrp)
                Sf = alloc('gSf', [128, 128])
                Sb = alloc('gSb', [128, 128], BF16)
                vnew = alloc('gvnew', [128, 128], BF16)
                hnt = hn_tmps()
                memset('dve', Sf, 0.0, [K('gSf')])
                memset('pool', Sb, 0.0, [K('gSb')])
                for n in range(NT):
                    tb = n // 4
                    nl = n % 4
                    nsl = slice(n * 128, (n + 1) * 128)
                    osl = slice(nl * 128, (nl + 1) * 128)
                    mm(PS[0][:, 0:128], WT[:, n, :], Sb, True, True, r=[K('gWT', n // 4), K('gSb')], w=[PSK[0]])
                    tt('dve', vnew, U[:, n, :], PS[0][:, 0:128], ALU.subtract, r=[K('gU', n // 4), PSK[0]], w=[K('gvnew')])
                    mm(PS[3][:, osl], Sb, qgT[:, nsl], True, False, r=[K('gSb'), K('gqg', n // 4)], w=[PSK[3]])
                    mm(PS[3][:, osl], vnew, attnT[:, n, :], False, True, r=[K('gvnew'), K('gattnT', n // 4)], w=[PSK[3]])
                    mm(PS[1][:, 0:128], kdtok[:, n, :], vnew, True, True, r=[K('gkdtok', n // 4), K('gvnew')], w=[PSK[1]])
                    stt('dve', Sf, Sf, sc['egl'][:, n, h:h + 1], PS[1][:, 0:128], ALU.mult, ALU.add, r=[K('gSf'), PSK[1], ka], w=[K('gSf')])
                    cp('act', Sb, Sf, r=[K('gSf')], w=[K('gSb')])
                    if nl == 3:
                        head_norm_gate(PS[3][:, :], PSK[3], wz, K('gwg_z'), slice(0, 128), tb, PR(l, 'gdng', 0, 1),
                                       oT[:, h, tb * 512:(tb + 1) * 512], K('gdn_oT', tb), hnt)
                release(m1)
            branch_proj(l, 1, oT, None, 4, False)
            release(m0)
        def dil_branch(l):
            m0 = mark()
            odT = alloc('dil_oT', [128, 2, T], BF16)
            numT = alloc('dnum', [128, 2, T], reg='x')
            denT = alloc('dden', [128, 2, T], reg='x')
            tC = alloc('dtC', [128, T])
            tS = alloc('dtS', [128, T])
            S.dma('sp', tC, tab_d[2], r=['tabs'], w=[K('dtC')])
            S.dma('sp', tS, tab_d[3], r=['tabs'], w=[K('dtS')])
            win_v = kp(w_in_d[l])
            for gi, dl in enumerate((1, 4, 16)):
                L = T // dl
                nsub = L // 128
                m1 = mark()

                def pblk(ap2, tb):
                    if dl == 1:
                        return ap2[:, tb * 512:(tb + 1) * 512]
                    v = ap2.rearrange("p (l r) -> p r l", r=dl)
                    if dl == 4:
                        return v[:, tb, :]
                    return v[:, 4 * tb:4 * tb + 4, :]

                def pblk_shape(ap512):
                    if dl == 16:
                        return ap512.rearrange("p (a b) -> p a b", a=4)
                    return ap512

                def psub(ap2, n):
                    if dl == 1:
                        return ap2[:, n * 128:(n + 1) * 128]
                    v = ap2.rearrange("p (l r) -> p r l", r=dl)
                    if dl == 4:
                        return v[:, n // 4, (n % 4) * 128:(n % 4 + 1) * 128]
                    return v[:, n, :]
                wq = alloc('dwq', [128, 8, 256], BF16)
                wk = alloc('dwk', [128, 8, 256], BF16)
                wv = alloc('dwv', [128, 8, 256], BF16)
                for w_, nm in ((wq, 'd_q'), (wk, 'd_k'), (wv, 'd_v')):
                    wload(w_, win_v[:, :, OFF[nm] + gi * 256:OFF[nm] + (gi + 1) * 256], K('dw' + nm))
                qT = alloc('dqT', [128, 2, T], BF16, reg='x')
                kT = alloc('dkT', [128, 2, T], BF16, reg='x')
                vg = alloc('dvg', [128, NT, 256], BF16, reg='x')
                P_ = alloc('dP', [128, 512], BF16)
                Pms = [alloc('dPm0', [128, 512], BF16), alloc('dPm1', [128, 512], BF16)]
                rot_t = (alloc('ro_xq', [128, 512]), alloc('ro_t1', [128, 512]), alloc('ro_t2', [128, 512]))
                for which, (w_, nm) in enumerate(((wq, 'd_q'), (wk, 'd_k'))):
                    for j in range(2):
                        for tb in range(NB):
                            for k in range(8):
                                mm(pblk_shape(PS[0][:, :]), w_[:, k, j * 128:(j + 1) * 128], pblk(xnT[:, k, :], tb), k == 0, k == 7,
                                   r=[K('dw' + nm), ('xnT', 0), ('xnT', 1), ('xnT', 2), ('xnT', 3)], w=[PSK[0]])
                            dst = (qT if which == 0 else kT)[:, j, tb * 512:(tb + 1) * 512]
                            xq, t1, t2 = rot_t
                            if which == 0:
                                S.op('act', lambda e: e.mul(xq, PS[0][:, :], 0.125), r=[PSK[0]], w=[K('ro_xq')])
                            else:
                                cp('act', xq, PS[0][:, :], r=[PSK[0]], w=[K('ro_xq')])
                            xqb = t2.bitcast(BF16)[:, 0:512]
                            cp('act', xqb, xq, r=[K('ro_xq')], w=[K('ro_t2')])
                            mm(PS[5][:, :], permd_b, xqb, True, True, r=[K('ro_t2'), 'negm'], w=[PSK[5]])
                            tt('dve', pblk_shape(t1), pblk_shape(PS[5][:, :]), pblk(tS, tb), ALU.mult, r=[PSK[5], K('dtS')], w=[K('ro_t1')])
                            tt('dve', pblk_shape(t2), pblk_shape(xq), pblk(tC, tb), ALU.mult, r=[K('ro_xq'), K('dtC')], w=[K('ro_t2')])
                            tt('dve', dst, t1, t2, ALU.add, r=[K('ro_t1'), K('ro_t2')], w=[K('dq' if which == 0 else 'dk', j, tb)])
                for n in range(NT):
                    for k in range(8):
                        mm(PS[1][:, 0:256], psub(xnT[:, k, :], n), wv[:, k, :], k == 0, k == 7,
                           r=[K('dwd_v'), ('xnT', 0), ('xnT', 1), ('xnT', 2), ('xnT', 3)], w=[PSK[1]])
                    cp('act', vg[:, n, :], PS[1][:, 0:256], r=[PSK[1]], w=[K('dvg', n)])
                for n in range(NT):
                    nsl = slice(n * 128, (n + 1) * 128)
                    kbs = [(n, 'm_own4')]
                    if n % nsub != 0:
                        kbs.append((n - 1, 'm_prev4'))
                    colh = lambda h_: ((h_ % 2) * 2 + h_ // 2) * 128
                    for bi, (kb, mname) in enumerate(kbs):
                        ksl = slice(kb * 128, (kb + 1) * 128)
                        banks = (2, 6) if bi == 0 else (5, 7)
                        for h in range(4):
                            rows = slice((h % 2) * 64, (h % 2 + 1) * 64)
                            bk = banks[h % 2]
                            mm(PS[bk][:, (h // 2) * 128:(h // 2 + 1) * 128], kT[rows, h // 2, ksl], qT[rows, h // 2, nsl], True, False,
                               r=[K('dk', h // 2, kb // 4), K('dq', h // 2, n // 4)], w=[PSK[bk]])
                            mm(PS[bk][:, (h // 2) * 128:(h // 2 + 1) * 128], identb, negm[0 if mname == 'm_own4' else 1], False, True,
                               r=['identb', 'negm'], w=[PSK[bk]])
                        for par in range(2):
                            act(Pms[bi][:, par * 256:(par + 1) * 256], PS[banks[par]][:, 0:256], AF.Exp, r=[PSK[banks[par]]], w=[K('dPm', bi)])
                    nkb = len(kbs)
                    for h in range(4):
                        for bi, (kb, mname) in enumerate(kbs):
                            mm(PS[3][:, h * 128:(h + 1) * 128], vg[:, kb, (h // 2) * 128:(h // 2 + 1) * 128], Pms[bi][:, colh(h):colh(h) + 128],
                               bi == 0, bi == nkb - 1, r=[K('dvg', kb), K('dPm', bi)], w=[PSK[3]])
                    for bi, (kb, mname) in enumerate(kbs):
                        mm(PS[4][:, :], onesb, Pms[bi], bi == 0, bi == nkb - 1, r=['onesb', K('dPm', bi)], w=[PSK[4]])
                    for h in range(4):
                        rows = slice((h % 2) * 64, (h % 2 + 1) * 64)
                        dn = psub(numT[rows, h // 2, :], n)
                        dd = psub(denT[rows, h // 2, :], n)
                        if gi == 0:
                            cp('act', dn, PS[3][rows, h * 128:(h + 1) * 128], r=[PSK[3]], w=[K('dnum')])
                            cp('dve', dd, PS[4][rows, colh(h):colh(h) + 128], r=[PSK[4]], w=[K('dden')])
                        else:
                            tt('dve', dn, dn, PS[3][rows, h * 128:(h + 1) * 128], ALU.add, r=[PSK[3], K('dnum')], w=[K('dnum')])
                            tt('dve', dd, dd, PS[4][rows, colh(h):colh(h) + 128], ALU.add, r=[PSK[4], K('dden')], w=[K('dden')])
                release(m1)
            for j in range(2):
                recip(denT[:, j, :], denT[:, j, :], r=[], w=[])
            S.barrier()
            for j in range(2):
                tt('dve', odT[:, j, :], numT[:, j, :], denT[:, j, :], ALU.mult, r=[], w=[])
            S.barrier()
            branch_proj(l, 2, odT, None, 2, False)
            release(m0)

        def mixer(l):
            rmsnorm_xn(PR(l, 'g_mix', 0, 8))
            S.barrier()
            for c_ in range(8):
                S.dma('sp', xsp_d[:, c_ * T:(c_ + 1) * T], xT[:, c_, :], r=[], w=['xsp'])
            S.barrier()
            m = mark()
            H['merged'] = alloc('merged', [128, 8, T], BF16)
            if 'r' in phases:
                ret_branch(l)
            else:
                for c_ in range(8):
                    memset('pool', H['merged'][:, c_, :], 0.0, [('merged', c_, t_) for t_ in range(4)])
            if 'g' in phases:
                gdn_branch(l)
            if 'd' in phases:
                dil_branch(l)
            S.barrier()
            for c_ in range(8):
                S.dma('sp', xT[:, c_, :], xsp_d[:, c_ * T:(c_ + 1) * T], r=['xsp'], w=[])
            S.barrier()
            wo_ = [alloc(f'wout{i}', [128, 8, 128], BF16) for i in range(2)]
            wv_ = kp(w_out_d[l])
            mg = H['merged']
            for c in range(8):
                i = c % 2
                wload(wo_[i], wv_[:, :, c * 128:(c + 1) * 128], K('wout', i))
                for tb in range(NB):
                    sl = slice(tb * 512, (tb + 1) * 512)
                    for k in range(8):
                        mm(PS[c % 2][:, :], wo_[i][:, k, :], mg[:, k, sl], k == 0, k == 7, r=[K('wout', i)], w=[PSK[c % 2]])
                    tt('dve', xT[:, c, sl], xT[:, c, sl], PS[c % 2][:, :], ALU.add, r=[PSK[c % 2], ('xT', c, tb)], w=[('xT', c, tb)])
            release(m)

        def xattn(l):
            rmsnorm_xn(PR(l, 'g_xat', 0, 8))
            m = mark()
            wq = alloc('xwq', [128, 8, 512], BF16)
            wkv = alloc('xwkv', [128, 8, 1024], BF16)
            wo = alloc('xwo', [128, 4, 1024], BF16)
            for k_ in range(8):
                wload(wq[:, k_, :], kp(wq_d[l])[:, k_, :], K('xwq'))
                wload(wkv[:, k_, :], kp(wkv_d[l])[:, k_, :], K('xwkv'))
            for k_ in range(4):
                wload(wo[:, k_, :], kp(wo_d[l])[:, k_, :], K('xwo'))
            memn = alloc('xmemn', [128, 8, 256], BF16)
            for c in range(8):
                ts('dve', memn[:, c, :], memh[:, c, :], PR(l, 'g_mem', c, c + 1), None, ALU.mult, None, r=['memh', 'prm'], w=[K('xmemn')])
            KT = alloc('xKT', [128, 4, 256], BF16)
            V = alloc('xV', [128, 2, 512], BF16)
            for h in range(4):
                for k in range(8):
                    mm(PS[0][:, 0:256], wkv[:, k, h * 128:(h + 1) * 128], memn[:, k, :], k == 0, k == 7, r=[K('xwkv'), K('xmemn')], w=[PSK[0]])
                cp('act', KT[:, h, :], PS[0][:, 0:256], r=[PSK[0]], w=[K('xKT')])
            for mt in range(2):
                for k in range(8):
                    mm(PS[1][:, :], memn[:, k, mt * 128:(mt + 1) * 128], wkv[:, k, 512:1024], k == 0, k == 7, r=[K('xwkv'), K('xmemn')], w=[PSK[1]])
                cp('act', V[:, mt, :], PS[1][:, :], r=[PSK[1]], w=[K('xV')])
            QT = alloc('xQT', [128, 512], BF16)
            Pq = alloc('xP', [128, 2, 512], BF16)
            rd = alloc('xrd', [128, 512])
            ox = alloc('xox', [128, 4, 512], BF16)
            for tb in range(NB):
                sl = slice(tb * 512, (tb + 1) * 512)
                for h in range(4):
                    projF(PS[0][:, :], PSK[0], wq, K('xwq'), slice(h * 128, (h + 1) * 128), xn_blk(tb), [('xnT', tb)])
                    S.op('act', lambda e: e.mul(QT, PS[0][:, :], float(128 ** -0.5)), r=[PSK[0]], w=[K('xQT')])
                    for mt in range(2):
                        mm(PS[1 + mt][:, :], KT[:, h, mt * 128:(mt + 1) * 128], QT, True, True, r=[K('xKT'), K('xQT')], w=[PSK[1 + mt]])
                        act(Pq[:, mt, :], PS[1 + mt][:, :], AF.Exp, r=[PSK[1 + mt]], w=[K('xP', mt)])
                    for mt in range(2):
                        mm(PS[3][:, :], V[:, mt, h * 128:(h + 1) * 128], Pq[:, mt, :], mt == 0, mt == 1, r=[K('xV'), K('xP', mt)], w=[PSK[3]])
                    for mt in range(2):
                        mm(PS[4][:, :], onesb, Pq[:, mt, :], mt == 0, mt == 1, r=['onesb', K('xP', mt)], w=[PSK[4]])
                    recip(rd, PS[4][:, :], r=[PSK[4]], w=[K('xrd')])
                    tt('dve', ox[:, h, :], PS[3][:, :], rd, ALU.mult, r=[PSK[3], K('xrd')], w=[K('xox', h)])
                for c in range(8):
                    for h in range(4):
                        mm(PS[5 + c % 2][:, :], wo[:, h, c * 128:(c + 1) * 128], ox[:, h, :], h == 0, h == 3, r=[K('xwo'), K('xox', h)], w=[PSK[5 + c % 2]])
                    tt('dve', xT[:, c, sl], xT[:, c, sl], PS[5 + c % 2][:, :], ALU.add, r=[PSK[5 + c % 2], ('xT', c, tb)], w=[('xT', c, tb)])
            release(m)

        def ffn(l):
            rmsnorm_xn(PR(l, 'g_ffn', 0, 8))
            m = mark()
            hT = alloc('fhT', [128, 11, T], BF16)
            wa = [alloc(f'fwa{i}', [128, 8, 128], BF16) for i in range(2)]
            wu = [alloc(f'fwu{i}', [128, 8, 128], BF16) for i in range(2)]
            wd = [alloc(f'fwd{i}', [128, 11, 128], BF16) for i in range(2)]
            rawa = [alloc(f'frawa{i}', [128, 514]) for i in range(2)]
            rawu = [alloc(f'frawu{i}', [128, 514]) for i in range(2)]
            accas = [alloc(f'facca{i}', [128, 512]) for i in range(2)]
            accus = [alloc(f'faccu{i}', [128, 512]) for i in range(2)]
            sas = [alloc(f'fsa{i}', [128, 512]) for i in range(2)]
            ptmp = alloc('fptmp', [128, 512])
            wup_v = kp(wup_d[l])
            wdn_v = kp(wdn_d[l])
            for half in range(2):
                for jl in range(11):
                    jj = half * 11 + jl
                    i = jl % 2
                    wload(wa[i], wup_v[:, :, jj * 128:(jj + 1) * 128], K('fwa', i))
                    wload(wu[i], wup_v[:, :, 2816 + jj * 128:2816 + (jj + 1) * 128], K('fwu', i))
                    for tb in range(NB):
                        sl = slice(tb * 512, (tb + 1) * 512)
                        info = []
                        acca, accu, sa = accas[tb % 2], accus[tb % 2], sas[tb % 2]
                        for which in range(2):
                            w_, wk_ = (wa[i], K('fwa', i)) if which == 0 else (wu[i], K('fwu', i))
                            raws = rawa if which == 0 else rawu
                            bk = which + 2 * (tb % 2)
                            info.append((raws, raws[tb % 2], K('fraw', which, tb % 2), jj if which == 0 else 22 + jj, bk))
                            projF(PS[bk][:, :], PSK[bk], w_, wk_, slice(0, 128), xn_blk(tb), [('xnT', tb)])
                        for which in range(2):
                            raws, rw, rk, cc_, bk = info[which]
                            if tb == 0:
                                memset('dve', rw[:, 0:2], 0.0, [rk])
                            else:
                                cp('act', rw[:, 0:2], raws[(tb - 1) % 2][:, 512:514], r=[K('fraw', which, (tb - 1) % 2)], w=[rk])
                            cp('act', rw[:, 2:514], PS[bk][:, :], r=[PSK[bk]], w=[rk])
                        for which in range(2):
                            raws, rw, rk, cc_, bk = info[which]
                            ac = acca if which == 0 else accu
                            ak = K('facc', which, tb % 2)
                            ts('dve', ac, rw[:, 2:514], PR(l, 'fconv', cc_ * 3 + 2, cc_ * 3 + 3), None, ALU.mult, None, r=[rk, 'prm'], w=[ak])
                            for t_ in range(0, 2):
                                stt('dve', ac, rw[:, t_:t_ + 512], PR(l, 'fconv', cc_ * 3 + t_, cc_ * 3 + t_ + 1), ac, ALU.mult, ALU.add,
                                    r=[rk, 'prm', ak], w=[ak])
                            if which == 0:
                                act(sa, ac, AF.Silu, r=[ak], w=[K('fsa', tb % 2)], bias=PR(l, 'fbias', cc_, cc_ + 1), scale=1.0)
                        raws, rw, rk, cc_, bk = info[1]
                        stt('dve', hT[:, jl, sl], accu, PR(l, 'fbias', cc_, cc_ + 1), sa, ALU.add, ALU.mult,
                            r=[K('facc', 1, tb % 2), K('fsa', tb % 2), 'prm'], w=[K('fhT', jl, tb)])
                for c in range(8):
                    i = c % 2
                    wload(wd[i], wdn_v[:, half * 11:(half + 1) * 11, c * 128:(c + 1) * 128], K('fwd', i))
                    for tb in range(NB):
                        sl = slice(tb * 512, (tb + 1) * 512)
                        for k in range(11):
                            mm(PS[4 + c % 2][:, :], wd[i][:, k, :], hT[:, k, sl], k == 0, k == 10, r=[K('fwd', i), K('fhT', k, tb)], w=[PSK[4 + c % 2]])
                        tt('dve', xT[:, c, sl], xT[:, c, sl], PS[4 + c % 2][:, :], ALU.add, r=[PSK[4 + c % 2], ('xT', c, tb)], w=[('xT', c, tb)])
            release(m)

        def seq_setup(s):
            m = mark()
            xin = [alloc(f'xin{i}', [128, DM]) for i in range(2)]
            for n in range(NT):
                i = n % 2
                S.dma('sp', xin[i], x_d[s, n * 128:(n + 1) * 128, :], r=[], w=[K('xin', i)])
                for half in range(2):
                    for cq in range(4):
                        c = half * 4 + cq
                        mm(PS[half][:, cq * 128:(cq + 1) * 128], xin[i][:, c * 128:(c + 1) * 128], identf, True, True, r=[K('xin', i), 'cst'], w=[PSK[half]])
                    cp('act' if half == 0 else 'dve', xT[:, half * 4:half * 4 + 4, n * 128:(n + 1) * 128],
                       PS[half][:, :].rearrange("p (a b) -> p a b", a=4), r=[PSK[half]],
                       w=[('xT', half * 4 + q_, n // 4) for q_ in range(4)])
            pi_ = alloc('pos_i', [128, T], I32)
            pf = alloc('pos_f', [128, T])
            u = alloc('tab_u', [128, T])
            ui = alloc('tab_ui', [128, T], I32)
            uf = alloc('tab_uf', [128, T])
            S.dma('sp', pi_, pos_d[s:s + 1, :].broadcast_to([128, T]), r=[], w=[K('pos_i')])
            cp('dve', pf, pi_, r=[K('pos_i')], w=[K('pos_f')])
            for ti in range(4):
                which = ti // 2
                iscos = (ti % 2 == 0)
                ts('dve', u, pf, C('invf', which, which + 1), float(1.0 / (2 * np.pi)), ALU.mult, ALU.mult, r=[K('pos_f'), 'cst'], w=[K('tab_u')])
                if iscos:
                    ts('dve', u, u, 0.25, None, ALU.add, None, r=[K('tab_u')], w=[K('tab_u')])
                cp('dve', ui, u, r=[K('tab_u')], w=[K('tab_ui')])
                cp('dve', uf, ui, r=[K('tab_ui')], w=[K('tab_uf')])
                tt('dve', u, u, uf, ALU.subtract, r=[K('tab_u'), K('tab_uf')], w=[K('tab_u')])
                stt('dve', uf, u, 0.5, u, ALU.is_gt, ALU.subtract, r=[K('tab_u')], w=[K('tab_uf')])
                act(uf, uf, AF.Sin, r=[K('tab_uf')], w=[K('tab_uf')], scale=float(-2 * np.pi))
                S.dma('sp', tab_d[ti], uf, r=[K('tab_uf')], w=['tabs'])
            mi = alloc('mem_in', [128, DM])
            junk = alloc('mem_junk', [128, DM])
            ss = alloc('mem_ss', [128, 1])
            for mt in range(2):
                S.dma('sp', mi, mem_d[s, mt * 128:(mt + 1) * 128, :], r=[], w=[K('mem_in')])
                act(junk, mi, AF.Square, r=[K('mem_in')], w=[K('mem_junk'), K('mem_ss')], accum_out=ss)
                act(ss, ss, AF.Sqrt, r=[K('mem_ss')], w=[K('mem_ss')], bias=EPS, scale=1.0 / DM)
                recip(ss, ss, r=[K('mem_ss')], w=[K('mem_ss')])
                ts('dve', mi, mi, ss[:, 0:1], None, ALU.mult, None, r=[K('mem_in'), K('mem_ss')], w=[K('mem_in')])
                for half in range(2):
                    for cq in range(4):
                        c = half * 4 + cq
                        mm(PS[2 + half][:, cq * 128:(cq + 1) * 128], mi[:, c * 128:(c + 1) * 128], identf, True, True, r=[K('mem_in'), 'cst'], w=[PSK[2 + half]])
                    cp('act', memh[:, half * 4:half * 4 + 4, mt * 128:(mt + 1) * 128],
                       PS[2 + half][:, :].rearrange("p (a b) -> p a b", a=4), r=[PSK[2 + half]], w=['memh'])
            release(m)

        def final_out(s):
            m = mark()
            sq = alloc('fo_sq', [128, 2, 512], BF16)
            rs = alloc('fo_rs', [128, 512])
            xo = alloc('fo_xo', [128, 8, 512])
            ot = [alloc(f'fo_ot{i}', [128, DM]) for i in range(2)]
            gcols = prm[:, 2 * PRM_L:2 * PRM_L + 8]
            for tb in range(NB):
                sl = slice(tb * 512, (tb + 1) * 512)
                for c in range(8):
                    act(sq[:, c % 2, :], xT[:, c, sl], AF.Square, r=[('xT', c, tb)], w=[K('fo_sq', c % 2)])
                    mm(PS[0][:, :], onesb, sq[:, c % 2, :], c == 0, c == 7, r=[K('fo_sq', c % 2), 'onesb'], w=[PSK[0]])
                act(rs, PS[0][:, :], AF.Sqrt, r=[PSK[0]], w=[K('fo_rs')], bias=EPS, scale=1.0 / DM)
                recip(rs, rs, r=[K('fo_rs')], w=[K('fo_rs')])
                for c in range(8):
                    stt('dve', xo[:, c, :], xT[:, c, sl], gcols[:, c:c + 1], rs, ALU.mult, ALU.mult,
                        r=[('xT', c, tb), K('fo_rs'), 'prm'], w=[K('fo_xo')])
                for nl in range(4):
                    n = tb * 4 + nl
                    i = n % 2
                    for half in range(2):
                        for cq in range(4):
                            c = half * 4 + cq
                            mm(PS[1 + half][:, cq * 128:(cq + 1) * 128], xo[:, c, nl * 128:(nl + 1) * 128], identf, True, True, r=[K('fo_xo'), 'cst'], w=[PSK[1 + half]])
                        cp('act' if half == 0 else 'dve', ot[i][:, half * 512:(half + 1) * 512], PS[1 + half][:, :], r=[PSK[1 + half]], w=[K('fo_ot', i)])
                    S.dma('sp', out_d[s, n * 128:(n + 1) * 128, :], ot[i], r=[K('fo_ot', i)], w=[])
            release(m)

        H = {}
        for s in range(nseq):
            seq_setup(s)
            for l in range(nlayer):
                if any(c_ in phases for c_ in 'rgdo'):
                    mixer(l)
                if 'x' in phases:
                    xattn(l)
                if 'f' in phases:
                    ffn(l)
            final_out(s)
        S.finish()
        print("instructions", S.ninst, "waits", S.nwaits, "arena top", astate['top'])
    return nc


_NC_CACHE = {}


def kernel(**inputs):
    inp = {k: np.asarray(v) for k, v in inputs.items()}
    if 'nc' not in _NC_CACHE:
        _NC_CACHE['nc'] = build(2, 2)
    nc = _NC_CACHE['nc']
    consts = make_consts()
    prm = make_prm(inp)
    shared = {k: np.ascontiguousarray(inp[k], dtype=np.float32) for k in
              ('w_in', 'w_br_ret', 'w_br_gdn', 'w_br_dil', 'w_out', 'xattn_wq', 'xattn_wkv', 'xattn_wo', 'ffn_w_up', 'ffn_w_down')}
    in_maps = []
    for c in range(8):
        d = dict(shared)
        d['x'] = np.ascontiguousarray(inp['x'][2 * c:2 * c + 2], dtype=np.float32)
        d['mem'] = np.ascontiguousarray(inp['mem'][2 * c:2 * c + 2], dtype=np.float32)
        d['positions'] = np.ascontiguousarray(inp['positions'][2 * c:2 * c + 2], dtype=np.int32)
        d['consts'] = consts
        d['prm'] = prm
        in_maps.append(d)
    res = run_bass_kernel_spmd(nc, in_maps, core_ids=list(range(8)))
    return np.concatenate([r['out'] for r in res.results], axis=0).astype(np.float32)
```

```python
import numpy as np
from contextlib import ExitStack
import concourse.bass as bass
import concourse.mybir as mybir
from concourse.bass_utils import run_bass_kernel_spmd

F32 = mybir.dt.float32
BF16 = mybir.dt.bfloat16
I32 = mybir.dt.int32
ALU = mybir.AluOpType
AF = mybir.ActivationFunctionType
AX = mybir.AxisListType


class Sched:
    NDS = 12
    EPOCH = 20000

    def __init__(self, nc, stack):
        self.nc = nc
        self.st = stack
        self.eng = {'pe': nc.tensor, 'act': nc.scalar, 'dve': nc.vector, 'pool': nc.gpsimd, 'sp': nc.sync}
        self.csem = {e: [] for e in ('pe', 'act', 'dve', 'pool')}
        self.cnt = {e: 0 for e in self.csem}
        self.dsem = {q: [stack.enter_context(nc.semaphore(f"d_{q}{i}")) for i in range(self.NDS)]
                     for q in ('sp', 'pool', 'act')}
        self.dval = {q: [0] * self.NDS for q in self.dsem}
        self.drr = {q: 0 for q in self.dsem}
        self.seen = {}
        self.lastw = {}
        self.reads = {}
        self.nwaits = 0
        self.ninst = 0
        self.scratch = None

    def sb(self, name, shape, dt):
        return self.st.enter_context(self.nc.sbuf_tensor(name, list(shape), dt))

    def ps(self, name, shape, dt):
        return self.st.enter_context(self.nc.psum_tensor(name, list(shape), dt))

    def _sem(self, key):
        if key[0] == 'd':
            return self.dsem[key[1]][key[2]]
        e, ep = key
        while len(self.csem[e]) <= ep:
            self.csem[e].append(self.st.enter_context(self.nc.semaphore(f"c_{e}{len(self.csem[e])}")))
        return self.csem[e][ep]

    def _wait(self, eng, tok):
        key, val = tok
        if key[0] != 'd':
            e, ep = key
            if e == 'pe' and eng == 'pe':
                return
            sk = (eng, e)
            cur = self.seen.get(sk, (-1, 0))
            if (ep, val) <= cur:
                return
            self.seen[sk] = (ep, val)
        else:
            sk = (eng, key)
            if self.seen.get(sk, 0) >= val:
                return
            self.seen[sk] = val
        self.eng[eng].wait_ge(self._sem(key), val)
        self.nwaits += 1

    def _deps(self, eng, r, w):
        toks = []
        for x in r:
            t = self.lastw.get(x)
            if t is not None:
                toks.append(t)
        for x in w:
            t = self.lastw.get(x)
            if t is not None:
                toks.append(t)
            toks.extend(self.reads.get(x, ()))
        for t in toks:
            self._wait(eng, t)

    def _record(self, tok, r, w):
        for x in r:
            self.reads.setdefault(x, []).append(tok)
        for x in w:
            self.lastw[x] = tok
            self.reads[x] = []

    @staticmethod
    def _excl(r, w):
        r2 = [x for x in r if not (isinstance(x, str) and x.startswith('ps'))]
        w2 = list(w) + [x for x in r if isinstance(x, str) and x.startswith('ps')]
        return r2, w2

    def _emit(self, eng, fn):
        ins = fn(self.eng[eng])
        ep, c = divmod(self.cnt[eng], self.EPOCH)
        key = (eng, ep)
        ins.then_inc(self._sem(key), 1)
        self.cnt[eng] += 1
        self.ninst += 1
        return (key, c + 1)

    def op(self, eng, fn, r=(), w=()):
        psr = [x for x in r if isinstance(x, str) and x.startswith('ps')]
        r, w = self._excl(r, w)
        self._deps(eng, r, w)
        tok = self._emit(eng, fn)
        if psr and eng in ('act', 'dve') and self.scratch is not None:
            self._record(tok, r, [x for x in w if x not in psr])
            sc = self.scratch[eng]
            self._deps(eng, [], ['scr_' + eng])
            tok2 = self._emit(eng, (lambda e: e.memzero(sc)) if eng == 'act' else (lambda e: e.memset(sc, 0.0)))
            self._record(tok2, [], psr + ['scr_' + eng])
            return tok
        self._record(tok, r, w)
        return tok

    def dma(self, q, out, in_, r=(), w=(), **kw):
        i = self.drr[q]
        self.drr[q] = (i + 1) % self.NDS
        key = ('d', q, i)
        if self.dval[q][i] > 0:
            self._wait(q, (key, self.dval[q][i]))
        self._deps(q, r, w)
        self.eng[q].dma_start(out=out, in_=in_, **kw).then_inc(self.dsem[q][i], 16)
        self.dval[q][i] += 16
        self.ninst += 1
        tok = (key, self.dval[q][i])
        self._record(tok, r, w)
        return tok

    def barrier(self):
        toks = []
        for q in self.dsem:
            for i in range(self.NDS):
                if self.dval[q][i] > 0:
                    toks.append((('d', q, i), self.dval[q][i]))
        for e in self.csem:
            if self.cnt[e] > 0:
                ep, c = divmod(self.cnt[e] - 1, self.EPOCH)
                toks.append(((e, ep), c + 1))
        for eng in ('pe', 'act', 'dve', 'pool', 'sp'):
            for t in toks:
                if t[0][0] == eng and eng != 'pe':
                    pass
                self._wait(eng, t)

    def finish(self):
        for q in self.dsem:
            for i in range(self.NDS):
                if self.dval[q][i] > 0:
                    self._wait('sp', (('d', q, i), self.dval[q][i]))
        for e in self.csem:
            if self.cnt[e] > 0:
                ep, c = divmod(self.cnt[e] - 1, self.EPOCH)
                self._wait('sp', ((e, ep), c + 1))


T = 2048
NT = 16
NB = 4
DM = 1024
EPS = 1e-6
NEGB = -30000.0
OFF = dict(r_q=0, r_k=256, r_v=512, r_g=1024, g_q=1536, g_k=2048, g_v=2560, g_a=3072, g_b=3076,
           g_z=3080, d_q=3592, d_k=4360, d_v=5128, gates=5896)
GAM = [1.0 - 2.0 ** (-5 - h) for h in range(4)]


def _const_layout():
    cols = {}
    o = 0
    for name, n in (('ident', 128), ('ones', 128), ('triT', 128), ('neg_su', 128), ('neg_ui', 128),
                    ('lvl', 7 * 128), ('m_own4', 512), ('m_prev4', 512), ('decT', 512), ('qdec', 256),
                    ('kdecT', 256), ('perm_ret', 128), ('perm_dil', 128), ('invf', 2)):
        cols[name] = (o, n)
        o += n
    return cols, o


CL, NCST = _const_layout()


def make_consts():
    c = np.zeros((128, NCST), np.float64)
    ar = np.arange(128)
    I, J = ar[:, None], ar[None, :]

    def put(name, a):
        o, n = CL[name]
        c[:, o:o + n] = a
    put('ident', (I == J) * 1.0)
    put('ones', np.ones((128, 128)))
    put('triT', (I <= J) * 1.0)
    put('neg_su', np.where(J > I, 0.0, NEGB))
    put('neg_ui', np.where(J >= I, 0.0, NEGB))
    lv = []
    for lb in range(7):
        b = 1 << lb
        lv.append(((I // (2 * b) == J // (2 * b)) & (I % (2 * b) >= b) & (J % (2 * b) < b)) * 1.0)
    put('lvl', np.concatenate(lv, axis=1))
    put('m_own4', np.tile((J >= I) * 1.0, (1, 4)))
    put('m_prev4', np.tile((J <= I) * 1.0, (1, 4)))
    lg = [np.log1p(-2.0 ** (-5 - h)) for h in range(4)]
    put('decT', np.concatenate([np.where(J >= I, np.exp(np.maximum(J - I, 0) * lg[h]), 0.0) for h in range(4)], axis=1))
    qd = np.zeros((128, 256))
    for j in range(2):
        for p in range(128):
            qd[p, j * 128:(j + 1) * 128] = np.exp((ar + 1.0) * lg[2 * j + p // 64])
    put('qdec', qd)
    kd = np.zeros((128, 256))
    for h in range(4):
        kd[:, h * 64:(h + 1) * 64] = np.exp((127.0 - ar) * lg[h])[:, None]
    put('kdecT', kd)
    pr = np.zeros((128, 128))
    pdl = np.zeros((128, 128))
    for m in range(128):
        d = m % 64
        if d < 32:
            pr[m + 32, m] = -1.0
        else:
            pr[m - 32, m] = 1.0
        if d < 8:
            pdl[m + 8, m] = -1.0
        elif d < 16:
            pdl[m - 8, m] = 1.0
    put('perm_ret', pr)
    put('perm_dil', pdl)
    ret_inv = 1.0 / (10000.0 ** np.linspace(0.0, 1.0, 32, dtype=np.float32).astype(np.float64))
    dil_inv = 500000.0 ** (-np.arange(0, 16, 2, dtype=np.float32).astype(np.float64) / 16)
    iv = np.zeros((128, 2))
    for m in range(128):
        d = m % 64
        iv[m, 0] = np.float32(ret_inv[d % 32])
        iv[m, 1] = np.float32(dil_inv[d % 8]) if d < 16 else 0.0
    put('invf', iv)
    return c.astype(np.float32)


PRM_L = 32 + 4 + 1 + 48 + 64 + 64 + 132 + 44
PO = dict(g_mix=0, g_xat=8, g_mem=16, g_ffn=24, retg=32, gdng=36, gconv=37, alog=85, dtb=149, fconv=213, fbias=345)
NPRM = 2 * PRM_L + 8


def make_prm(inp):
    p = np.zeros((128, NPRM), np.float32)

    def fm(v):
        return np.ascontiguousarray(v.reshape(8, 128).T)
    for l in range(2):
        b = l * PRM_L
        p[:, b + 0:b + 8] = fm(inp['norm_mix_g'][l])
        p[:, b + 8:b + 16] = fm(inp['norm_xattn_g'][l])
        p[:, b + 16:b + 24] = fm(inp['norm_mem_g'][l])
        p[:, b + 24:b + 32] = fm(inp['norm_ffn_g'][l])
        p[:, b + 32:b + 36] = inp['ret_norm_g'][l].T
        p[:, b + 36] = inp['gdn_norm_g'][l]
        p[:, b + 37:b + 85] = inp['gdn_conv_w'][l].reshape(4, 12, 128).transpose(2, 1, 0).reshape(128, 48)
        p[:, b + 85:b + 149] = np.tile(inp['gdn_a_log'][l][None, :], (128, 16))
        p[:, b + 149:b + 213] = np.tile(inp['gdn_dt_bias'][l][None, :], (128, 16))
        p[:, b + 213:b + 345] = inp['ffn_conv_w'][l].reshape(3, 44, 128).transpose(2, 1, 0).reshape(128, 132)
        p[:, b + 345:b + 389] = inp['ffn_conv_b'][l].reshape(44, 128).T
    p[:, 2 * PRM_L:2 * PRM_L + 8] = fm(inp['final_norm_g'])
    return p
def build(nseq=2, nlayer=2, dbg=False, phases='rgdoxf'):
    nc = bass.Bass("TRN2", target_bir_lowering=False)
    D = {}
    def din(name, shape, dt=F32):
        D[name] = nc.dram_tensor(name, list(shape), dt, kind="ExternalInput").ap()
        return D[name]
    x_d = din('x', [nseq, T, DM])
    mem_d = din('mem', [nseq, 256, DM])
    pos_d = din('positions', [nseq, T], I32)
    cst_d = din('consts', [128, NCST])
    prm_d = din('prm', [128, NPRM])
    w_in_d = din('w_in', [2, DM, 8968])
    wbr_d = [din('w_br_ret', [2, 512, DM]), din('w_br_gdn', [2, 512, DM]), din('w_br_dil', [2, 256, DM])]
    w_out_d = din('w_out', [2, DM, DM])
    wq_d = din('xattn_wq', [2, DM, 512])
    wkv_d = din('xattn_wkv', [2, DM, 1024])
    wo_d = din('xattn_wo', [2, 512, DM])
    wup_d = din('ffn_w_up', [2, DM, 5632])
    wdn_d = din('ffn_w_down', [2, 2816, DM])
    out_d = nc.dram_tensor('out', [nseq, T, DM], F32, kind="ExternalOutput").ap()
    xsp_d = nc.dram_tensor('xspill', [128, 8 * T], F32, kind="ExternalOutput").ap()
    tab_d = nc.dram_tensor('tabs', [4, 128, T], F32, kind="ExternalOutput").ap()
    dbg_d = nc.dram_tensor('dbg', [128, 8 * T], F32, kind="ExternalOutput").ap() if dbg else None

    def kp(ap2d):
        return ap2d.rearrange("(k p) n -> p k n", p=128)

    with ExitStack() as st:
        S = Sched(nc, st)
        ARW = 53000
        arena = S.sb("arena", [128, ARW], F32)
        PS = [S.ps(f"ps{i}", [128, 512], F32) for i in range(8)]
        S.scratch = {'act': S.sb('scr_act', [128, 2], F32)[:], 'dve': S.sb('scr_dve', [128, 2], F32)[:]}
        PSK = [f"ps{i}" for i in range(8)]
        astate = {'top': 0, 'gen': 0}

        def alloc(name, shape, dt=F32, reg='m'):
            n = 1
            for s_ in shape[1:]:
                n *= s_
            words = n if dt != BF16 else (n + 1) // 2
            if reg == 'x':
                o = astate['xtop']
                assert o + words <= astate['xend'], (name, o, words)
                astate['xtop'] = o + words
            else:
                o = astate['top']
                assert o + words <= ARW, (name, o, words)
                astate['top'] = o + words
            ap = arena[0:shape[0], o:o + words]
            if dt != F32:
                ap = ap.bitcast(dt)
            if dt == BF16 and n % 2:
                ap = ap[:, 0:n]
            if len(shape) == 3:
                ap = ap.rearrange("p (a b) -> p a b", a=shape[1])
            elif len(shape) == 4:
                ap = ap.rearrange("p (a b c) -> p a b c", a=shape[1], b=shape[2])
            return ap

        def run_gens(gens):
            gens = list(gens)
            while gens:
                for g_ in list(gens):
                    try:
                        next(g_)
                    except StopIteration:
                        gens.remove(g_)

        def mark():
            return (astate['top'], astate.get('xtop', 0))

        def release(m):
            astate['top'] = m[0]
            astate['xtop'] = m[1]
            astate['gen'] += 1
            S.barrier()

        def K(name, *idx):
            return (name, astate['gen']) + idx

        def mm(out, lhsT, rhs, start, stop, r, w):
            S.op('pe', lambda e: e.matmul(out, lhsT, rhs, start=start, stop=stop), r=r, w=w)

        def tr(out, in_, ident, r, w):
            S.op('pe', lambda e: e.transpose(out, in_, ident), r=r, w=w)

        def act(out, in_, func, r, w, **kw):
            S.op('act', lambda e: e.activation(out, in_, func, **kw), r=r, w=w)

        def tt(eng, out, a, b, op, r, w):
            S.op(eng, lambda e: e.tensor_tensor(out, a, b, op), r=r, w=w)

        def ts(eng, out, a, s1, s2, op0, op1, r, w):
            if op1 is None:
                S.op(eng, lambda e: e.tensor_scalar(out, a, s1, None, op0), r=r, w=w)
            else:
                S.op(eng, lambda e: e.tensor_scalar(out, a, s1, s2, op0, op1), r=r, w=w)

        def stt(eng, out, a, sc, b, op0, op1, r, w):
            S.op(eng, lambda e: e.scalar_tensor_tensor(out, a, sc, b, op0, op1), r=r, w=w)

        def cp(eng, out, in_, r, w):
            if eng == 'act':
                S.op('act', lambda e: e.copy(out, in_), r=r, w=w)
            else:
                S.op(eng, lambda e: e.tensor_copy(out, in_), r=r, w=w)

        def memset(eng, ap, val, w):
            S.op(eng, lambda e: e.memset(ap, val), r=[], w=w)

        def recip(out, in_, r, w):
            S.op('dve', lambda e: e.reciprocal(out, in_), r=r, w=w)

        cst = alloc('cst', [128, NCST])
        prm = alloc('prm', [128, NPRM])
        identb = alloc('identb', [128, 128], BF16)
        onesb = alloc('onesb', [128, 128], BF16)
        astate['xtop'] = astate['top']
        xT = alloc('xT', [128, 8, T])
        astate['xend'] = astate['top']
        xnT = alloc('xnT', [128, 8, T], BF16)
        memh = alloc('memh', [128, 8, 256])
        S.dma('sp', cst, cst_d, r=[], w=['cst'])
        S.dma('sp', prm, prm_d, r=[], w=['prm'])

        def C(name, a=None, b=None):
            o, n = CL[name]
            if a is None:
                return cst[:, o:o + n]
            return cst[:, o + a:o + b]
        cp('dve', identb, C('ident'), r=['cst'], w=['identb'])
        cp('dve', onesb, C('ones'), r=['cst'], w=['onesb'])
        negm = [alloc('negown', [128, 128], BF16), alloc('negprev', [128, 128], BF16)]
        negsu_b = alloc('negsu_b', [128, 128], BF16)
        negui_b = alloc('negui_b', [128, 128], BF16)
        permr_b = alloc('permr_b', [128, 128], BF16)
        permd_b = alloc('permd_b', [128, 128], BF16)
        cp('dve', negsu_b, C('neg_su'), r=['cst'], w=['negm'])
        cp('dve', negui_b, C('neg_ui'), r=['cst'], w=['negm'])
        cp('dve', permr_b, C('perm_ret'), r=['cst'], w=['negm'])
        cp('dve', permd_b, C('perm_dil'), r=['cst'], w=['negm'])
        ts('dve', negm[0], C('m_own4', 0, 128), 1.0, 30000.0, ALU.subtract, ALU.mult, r=['cst'], w=['negm'])
        ts('dve', negm[1], C('m_prev4', 0, 128), 1.0, 30000.0, ALU.subtract, ALU.mult, r=['cst'], w=['negm'])
        identf = C('ident')
        onesf = C('ones')
        base_mark = mark()

        def wload(dst, src, key):
            S.dma('pool', dst, src, r=[], w=[key])

        def PR(l, name, a, b):
            o = l * PRM_L + PO[name]
            return prm[:, o + a:o + b]

        def rmsnorm_xn(gcols):
            m = mark()
            sq = alloc('sq', [128, 2, 512], BF16)
            rs = alloc('rs', [128, 512])
            for tb in range(NB):
                sl = slice(tb * 512, (tb + 1) * 512)
                for c in range(8):
                    act(sq[:, c % 2, :], xT[:, c, sl], AF.Square, r=[('xT', c, tb)], w=[K('sq', c % 2)])
                    mm(PS[0][:, :], onesb, sq[:, c % 2, :], c == 0, c == 7, r=[K('sq', c % 2), 'onesb'], w=[PSK[0]])
                act(rs, PS[0][:, :], AF.Sqrt, r=[PSK[0]], w=[K('rs')], bias=EPS, scale=1.0 / DM)
                recip(rs, rs, r=[K('rs')], w=[K('rs')])
                for c in range(8):
                    stt('dve', xnT[:, c, sl], xT[:, c, sl], gcols[:, c:c + 1], rs, ALU.mult, ALU.mult,
                        r=[('xT', c, tb), K('rs'), 'prm'], w=[('xnT', tb)])
            release(m)

        def projF(ps, psk, w, wk, cols, rhs_fn, rkeys):
            for k in range(8):
                mm(ps, w[:, k, cols], rhs_fn(k), k == 0, k == 7, r=[wk] + rkeys, w=[psk])

        def xn_blk(tb):
            return lambda k: xnT[:, k, tb * 512:(tb + 1) * 512]

        def branch_proj(l, b, oT, okey, nk, first):
            m = mark()
            wb = [alloc(f'wb{i}', [128, nk, 128], BF16) for i in range(2)]
            wg = [alloc(f'wg{i}', [128, 8, 128], BF16) for i in range(2)]
            G = alloc('G', [128, 512])
            tmp = alloc('bp_tmp', [128, 512], BF16)
            wbr_v = kp(wbr_d[b][l])
            win_v = kp(w_in_d[l])
            for c in range(8):
                i = c % 2
                wload(wb[i], wbr_v[:, :, c * 128:(c + 1) * 128], K('wb', i))
                wload(wg[i], win_v[:, :, OFF['gates'] + b * 1024 + c * 128:OFF['gates'] + b * 1024 + (c + 1) * 128], K('wg', i))
                for tb in range(NB):
                    sl = slice(tb * 512, (tb + 1) * 512)
                    for k in range(nk):
                        mm(PS[0][:, :], wb[i][:, k, :], oT[:, k, sl], k == 0, k == nk - 1, r=[K('wb', i), okey], w=[PSK[0]])
                    projF(PS[1][:, :], PSK[1], wg[i], K('wg', i), slice(0, 128), xn_blk(tb), [('xnT', tb)])
                    act(G, PS[1][:, :], AF.Sigmoid, r=[PSK[1]], w=[K('G')])
                    if first:
                        tt('dve', H['merged'][:, c, sl], PS[0][:, :], G, ALU.mult, r=[PSK[0], K('G')], w=[('merged', c, tb)])
                    else:
                        tt('dve', tmp, PS[0][:, :], G, ALU.mult, r=[PSK[0], K('G')], w=[K('bp_tmp')])
                        tt('dve', H['merged'][:, c, sl], H['merged'][:, c, sl], tmp, ALU.add, r=[K('bp_tmp'), ('merged', c, tb)], w=[('merged', c, tb)])
            release(m)

        def head_norm_gate(pso, psok, wg, wgk, gcols, tb, gcol, dst, dkey, tmpk):
            sq, rs, sg, t1 = tmpk
            act(sq, pso, AF.Square, r=[psok], w=[K('hn_sq')])
            mm(PS[6][:, :], onesb, sq, True, True, r=[K('hn_sq'), 'onesb'], w=[PSK[6]])
            act(rs, PS[6][:, :], AF.Sqrt, r=[PSK[6]], w=[K('hn_rs')], bias=EPS, scale=1.0 / 128)
            recip(rs, rs, r=[K('hn_rs')], w=[K('hn_rs')])
            projF(PS[7][:, :], PSK[7], wg, wgk, gcols, xn_blk(tb), [('xnT', tb)])
            act(sg, PS[7][:, :], AF.Silu, r=[PSK[7]], w=[K('hn_sg')])
            tt('dve', t1, pso, rs, ALU.mult, r=[psok, K('hn_rs')], w=[K('hn_t1')])
            stt('dve', dst, t1, gcol, sg, ALU.mult, ALU.mult, r=[K('hn_t1'), K('hn_sg'), 'prm'], w=[dkey])

        def hn_tmps():
            return (alloc('hn_sq', [128, 512], BF16), alloc('hn_rs', [128, 512]), alloc('hn_sg', [128, 512]),
                    alloc('hn_t1', [128, 512]))

        def rotary_block(ps_src, psk_src, perm, tabC, tabS, tkeys, scale, dst, dkey, tmps):
            xq, t1, t2 = tmps
            if scale == 1.0:
                cp('act', xq, ps_src, r=[psk_src], w=[K('ro_xq')])
            else:
                S.op('act', lambda e: e.mul(xq, ps_src, float(scale)), r=[psk_src], w=[K('ro_xq')])
            xqb = t2.bitcast(BF16)[:, 0:512]
            cp('act', xqb, xq, r=[K('ro_xq')], w=[K('ro_t2')])
            mm(PS[5][:, :], perm, xqb, True, True, r=[K('ro_t2'), 'negm'], w=[PSK[5]])
            tt('dve', t1, PS[5][:, :], tabS, ALU.mult, r=[PSK[5]] + tkeys, w=[K('ro_t1')])
            tt('dve', t2, xq, tabC, ALU.mult, r=[K('ro_xq')] + tkeys, w=[K('ro_t2')])
            tt('dve', dst, t1, t2, ALU.add, r=[K('ro_t1'), K('ro_t2')], w=[dkey])

        def ret_branch(l):
            m0 = mark()
            oT = alloc('ret_oT', [128, 4, T], BF16, reg='x')
            win_v = kp(w_in_d[l])
            for j in range(2):
                m1 = mark()
                wq = alloc('rwq', [128, 8, 128], BF16, reg='x')
                wk = alloc('rwk', [128, 8, 128], BF16, reg='x')
                wv = alloc('rwv', [128, 8, 256], BF16, reg='x')
                wg = alloc('rwg', [128, 8, 256], BF16, reg='x')
                wload(wq, win_v[:, :, OFF['r_q'] + j * 128:OFF['r_q'] + (j + 1) * 128], K('rwq'))
                wload(wk, win_v[:, :, OFF['r_k'] + j * 128:OFF['r_k'] + (j + 1) * 128], K('rwk'))
                wload(wv, win_v[:, :, OFF['r_v'] + j * 256:OFF['r_v'] + (j + 1) * 256], K('rwv'))
                wload(wg, win_v[:, :, OFF['r_g'] + j * 256:OFF['r_g'] + (j + 1) * 256], K('rwg'))
                qT = alloc('rqT', [128, T], BF16, reg='x')
                kT = alloc('rkT', [128, T], BF16, reg='x')
                qdT = alloc('rqdT', [128, T], BF16, reg='x')
                vtok = alloc('rv', [128, NT, 256], BF16, reg='x')
                kd = alloc('rkd', [128, NT, 128], BF16, reg='x')
                tC = alloc('rtC', [128, 512])
                tS = alloc('rtS', [128, 512])
                rot_t = (alloc('ro_xq', [128, 512]), alloc('ro_t1', [128, 512]), alloc('ro_t2', [128, 512]))
                for tb in range(NB):
                    sl = slice(tb * 512, (tb + 1) * 512)
                    S.dma('sp', tC, tab_d[0, :, sl], r=['tabs'], w=[K('rtC')])
                    S.dma('sp', tS, tab_d[1, :, sl], r=['tabs'], w=[K('rtS')])
                    for which in range(2):
                        w_, wk_ = (wq, K('rwq')) if which == 0 else (wk, K('rwk'))
                        projF(PS[0][:, :], PSK[0], w_, wk_, slice(0, 128), xn_blk(tb), [('xnT', tb)])
                        dst = qT if which == 0 else kT
                        rotary_block(PS[0][:, :], PSK[0], permr_b, tC, tS, [K('rtC'), K('rtS')],
                                     1.0 if which == 0 else 0.125, dst[:, sl], K('rq' if which == 0 else 'rk', tb), rot_t)
                    for n in range(tb * 4, tb * 4 + 4):
                        nsl = slice(n * 128, (n + 1) * 128)
                        tt('dve', qdT[:, nsl], qT[:, nsl], C('qdec', j * 128, (j + 1) * 128), ALU.mult,
                           r=[K('rq', tb), 'cst'], w=[K('rqd', n)])
                        for k in range(8):
                            mm(PS[1][:, 0:256], xnT[:, k, nsl], wv[:, k, :], k == 0, k == 7, r=[('xnT', tb), K('rwv')], w=[PSK[1]])
                        cp('act', vtok[:, n, :], PS[1][:, 0:256], r=[PSK[1]], w=[K('rv', n)])
                        pb = PS[2][:, 0:64].bitcast(BF16)
                        tr(pb, kT[:, nsl], identb, r=[K('rk', tb), 'identb'], w=[PSK[2]])
                        tt('dve', kd[:, n, :], pb, C('kdecT', j * 128, (j + 1) * 128), ALU.mult, r=[PSK[2], 'cst'], w=[K('rkd', n)])
                st_f = alloc('rst_f', [128, 128])
                st_b = alloc('rst_b', [128, 128], BF16)
                STs = alloc('rSTs', [128, 2, 128], BF16)
                hnt = hn_tmps()
                memset('dve', st_f, 0.0, [K('rst_f')])
                memset('pool', st_b, 0.0, [K('rst_b')])
                for tb in range(NB):
                    sl = slice(tb * 512, (tb + 1) * 512)
                    for nl in range(4):
                        n = tb * 4 + nl
                        nsl = slice(n * 128, (n + 1) * 128)
                        osl = slice(nl * 128, (nl + 1) * 128)
                        for a in range(2):
                            h = 2 * j + a
                            rows = slice(a * 64, (a + 1) * 64)
                            bk = 2 if a == 0 else 5
                            mm(PS[bk][:, 0:128], kT[rows, nsl], qT[rows, nsl], True, True,
                               r=[K('rk', tb), K('rq', tb)], w=[PSK[bk]])
                        for a in range(2):
                            bk = 2 if a == 0 else 5
                            tt('dve', STs[:, a, :], PS[bk][:, 0:128],
                               C('decT', (2 * j + a) * 128, (2 * j + a + 1) * 128), ALU.mult, r=[PSK[bk], 'cst'], w=[K('rSTs')])
                        for a in range(2):
                            rows = slice(a * 64, (a + 1) * 64)
                            mm(PS[3 + a][:, osl], vtok[:, n, a * 128:(a + 1) * 128], STs[:, a, :], True, False,
                               r=[K('rv', n), K('rSTs')], w=[PSK[3 + a]])
                            mm(PS[3 + a][:, osl], st_b[rows, :], qdT[rows, nsl], False, True,
                               r=[K('rst_b'), K('rqd', n)], w=[PSK[3 + a]])
                        mm(PS[1][:, 0:256], kd[:, n, :], vtok[:, n, :], True, True, r=[K('rkd', n), K('rv', n)], w=[PSK[1]])
                        for a in range(2):
                            rows = slice(a * 64, (a + 1) * 64)
                            stt('dve', st_f[rows, :], st_f[rows, :], float(GAM[2 * j + a] ** 128), PS[1][rows, a * 128:(a + 1) * 128],
                                ALU.mult, ALU.add, r=[K('rst_f'), PSK[1]], w=[K('rst_f')])
                        cp('act', st_b, st_f, r=[K('rst_f')], w=[K('rst_b')])
                    for a in range(2):
                        h = 2 * j + a
                        head_norm_gate(PS[3 + a][:, :], PSK[3 + a], wg, K('rwg'), slice(a * 128, (a + 1) * 128), tb,
                                       PR(l, 'retg', h, h + 1), oT[:, h, sl], K('ret_oT', tb), hnt)
                release(m1)
            branch_proj(l, 0, oT, None, 4, True)
            release(m0)
        def gdn_branch(l):
            m0 = mark()
            oT = alloc('gdn_oT', [128, 4, T], BF16, reg='x')
            win_v = kp(w_in_d[l])
            wab = alloc('gwab', [128, 8, 8], BF16)
            wload(wab, win_v[:, :, OFF['g_a']:OFF['g_a'] + 8], K('gwab'))
            AB = alloc('gAB', [128, NT, 8])
            for n in range(NT):
                for k in range(8):
                    mm(PS[0][:, n * 8:(n + 1) * 8], xnT[:, k, n * 128:(n + 1) * 128], wab[:, k, :], k == 0, k == 7,
                       r=[('xnT', n // 4), K('gwab')], w=[PSK[0]])
            cp('act', AB.rearrange("p a b -> p (a b)"), PS[0][:, 0:128], r=[PSK[0]], w=[K('gAB')])
            names = ['g', 'lnb', 'gc', 'gtot', 'rr', 'bg', 'beta', 'kdecs', 'egl', 'tmpa', 'negA', 'ngc', 'rr_h', 'rr_l', 'gc_h', 'gc_l', 'ngc_h', 'ngc_l']
            sc = {nm: alloc('gs_' + nm, [128, NT, 4]) for nm in names}
            f2 = lambda a: a.rearrange("p a b -> p (a b)")
            ka = K('gsc')
            act(f2(sc['negA']), PR(l, 'alog', 0, 64), AF.Exp, r=['prm'], w=[ka])
            tt('dve', sc['tmpa'], AB[:, :, 0:4], PR(l, 'dtb', 0, 64).rearrange("p (a b) -> p a b", b=4), ALU.add, r=[K('gAB'), 'prm', ka], w=[ka])
            act(f2(sc['tmpa']), f2(sc['tmpa']), AF.Exp, r=[ka], w=[ka])
            act(f2(sc['tmpa']), f2(sc['tmpa']), AF.Ln, r=[ka], w=[ka], bias=1.0)
            stt('dve', f2(sc['g']), f2(sc['tmpa']), -1.0, f2(sc['negA']), ALU.mult, ALU.mult, r=[ka], w=[ka])
            act(sc['lnb'], AB[:, :, 4:8], AF.Exp, r=[K('gAB'), ka], w=[ka], scale=-1.0)
            act(f2(sc['lnb']), f2(sc['lnb']), AF.Ln, r=[ka], w=[ka], bias=1.0)
            ts('dve', f2(sc['lnb']), f2(sc['lnb']), -1.0, None, ALU.mult, None, r=[ka], w=[ka])
            mm(PS[1][:, 0:64], C('triT'), f2(sc['g']), True, True, r=[ka, 'cst'], w=[PSK[1]])
            mm(PS[1][:, 64:128], onesf, f2(sc['g']), True, True, r=[ka, 'cst'], w=[PSK[1]])
            cp('act', f2(sc['gc']), PS[1][:, 0:64], r=[PSK[1]], w=[ka])
            cp('act', f2(sc['gtot']), PS[1][:, 64:128], r=[PSK[1]], w=[ka])
            tt('dve', f2(sc['rr']), f2(sc['gc']), f2(sc['lnb']), ALU.add, r=[ka], w=[ka])
            ts('dve', f2(sc['ngc']), f2(sc['gc']), -1.0, None, ALU.mult, None, r=[ka], w=[ka])
            act(f2(sc['bg']), f2(sc['rr']), AF.Exp, r=[ka], w=[ka])
            act(f2(sc['beta']), f2(sc['lnb']), AF.Exp, r=[ka], w=[ka])
            tt('dve', f2(sc['kdecs']), f2(sc['gtot']), f2(sc['gc']), ALU.subtract, r=[ka], w=[ka])
            act(f2(sc['kdecs']), f2(sc['kdecs']), AF.Exp, r=[ka], w=[ka])
            act(f2(sc['egl']), f2(sc['gtot']), AF.Exp, r=[ka], w=[ka])
            hb = alloc('gs_hb', [128, 64], BF16)
            for nm_ in ('rr', 'gc'):
                cp('dve', hb, f2(sc[nm_]), r=[ka], w=[ka])
                cp('dve', f2(sc[nm_ + '_h']), hb, r=[ka], w=[ka])
                tt('dve', f2(sc[nm_ + '_l']), f2(sc[nm_]), f2(sc[nm_ + '_h']), ALU.subtract, r=[ka], w=[ka])
                cp('dve', hb, f2(sc[nm_ + '_l']), r=[ka], w=[ka])
                cp('dve', f2(sc[nm_ + '_l']), hb, r=[ka], w=[ka])
            ts('dve', f2(sc['ngc_h']), f2(sc['gc_h']), -1.0, None, ALU.mult, None, r=[ka], w=[ka])
            ts('dve', f2(sc['ngc_l']), f2(sc['gc_l']), -1.0, None, ALU.mult, None, r=[ka], w=[ka])
            S.barrier()
            for h in range(4):
                m1 = mark()
                wq = alloc('gwq', [128, 8, 128], BF16)
                wk = alloc('gwk', [128, 8, 128], BF16)
                wv = alloc('gwv', [128, 8, 128], BF16)
                wz = alloc('gwz', [128, 8, 128], BF16)
                for w_, nm in ((wq, 'g_q'), (wk, 'g_k'), (wv, 'g_v'), (wz, 'g_z')):
                    wload(w_, win_v[:, :, OFF[nm] + h * 128:OFF[nm] + (h + 1) * 128], K('gw' + nm))
                qT = alloc('gqT', [128, T], BF16, reg='x')
                kT = alloc('gkT', [128, T], BF16, reg='x')
                qgT = alloc('gqgT', [128, T], BF16, reg='x')
                ktok = alloc('gktok', [128, NT, 128], BF16, reg='x')
                vtok = alloc('gvtok', [128, NT, 128], BF16, reg='x')
                kdtok = alloc('gkdtok', [128, NT, 128], BF16, reg='x')
                U = alloc('gU', [128, NT, 128], reg='x')
                WT = alloc('gWT', [128, NT, 128], BF16, reg='x')
                attnT = alloc('gattnT', [128, NT, 128], BF16, reg='x')
                mconv = mark()
                wlist = ((wq, 'g_q'), (wk, 'g_k'), (wv, 'g_v'))
                banksets = ((0, 1, 2, None), (3, 4, 5, 4), (6, 7, None, 7))
                dws = []
                for which in range(3):
                    row = []
                    for jj in range(4):
                        dw_ = alloc(f'gdw{which}{jj}', [128, 128], BF16)
                        wc_ = PR(l, 'gconv', (which * 4 + h) * 4 + jj, (which * 4 + h) * 4 + jj + 1)
                        S.op('act', (lambda e, dw_=dw_, wc_=wc_: e.mul(dw_, identb, wc_)), r=['identb', 'prm'], w=[K('gdw')])
                        row.append(dw_)
                    dws.append(row)

                def conv_gen(which):
                    w_, nm = wlist[which]
                    bp, bc_, bo, bt = banksets[which]
                    raw = [alloc(f'graw{which}{i}', [128, 516], BF16) for i in range(2)]
                    cc = alloc(f'gcc{which}', [128, 512])
                    sq = alloc(f'gsq{which}', [128, 512], BF16)
                    rs = alloc(f'grs{which}', [128, 512]) if which < 2 else None
                    yield
                    for tb in range(NB):
                        sl = slice(tb * 512, (tb + 1) * 512)
                        rw = raw[tb % 2]
                        rk = K('graw', which, tb % 2)
                        projF(PS[bp][:, :], PSK[bp], w_, K('gw' + nm), slice(0, 128), xn_blk(tb), [('xnT', tb)])
                        yield
                        if tb == 0:
                            memset('dve', rw[:, 0:3], 0.0, [rk])
                        else:
                            cp('act', rw[:, 0:3], raw[(tb - 1) % 2][:, 512:515], r=[K('graw', which, (tb - 1) % 2)], w=[rk])
                        cp('act', rw[:, 3:515], PS[bp][:, :], r=[PSK[bp]], w=[rk])
                        yield
                        for jj in range(4):
                            mm(PS[bc_][:, :], dws[which][jj], rw[:, jj:jj + 512], jj == 0, jj == 3, r=[rk, K('gdw')], w=[PSK[bc_]])
                        yield
                        act(cc, PS[bc_][:, :], AF.Silu, r=[PSK[bc_]], w=[K('gcc', which)])
                        yield
                        if which < 2:
                            act(sq, cc, AF.Square, r=[K('gcc', which)], w=[K('gsq', which)])
                            yield
                            mm(PS[bo][:, :], onesb, sq, True, True, r=[K('gsq', which), 'onesb'], w=[PSK[bo]])
                            yield
                            act(rs, PS[bo][:, :], AF.Sqrt, r=[PSK[bo]], w=[K('grs', which)], bias=EPS, scale=1.0)
                            yield
                            recip(rs, rs, r=[K('grs', which)], w=[K('grs', which)])
                            dst = qT if which == 0 else kT
                            dk_ = K('gq' if which == 0 else 'gk', tb)
                            stt('dve', dst[:, sl], cc, float(128 ** -0.5) if which == 0 else 1.0, rs, ALU.mult, ALU.mult,
                                r=[K('gcc', which), K('grs', which)], w=[dk_])
                            src = dst[:, sl]
                            skey = dk_
                            yield
                        else:
                            cp('act', sq, cc, r=[K('gcc', which)], w=[K('gsq', which)])
                            src = sq
                            skey = K('gsq', which)
                            yield
                        if which >= 1:
                            pb = PS[bt][:, 0:256].bitcast(BF16)
                            for i in range(4):
                                tr(pb[:, i * 128:(i + 1) * 128], src[:, i * 128:(i + 1) * 128], identb, r=[skey, 'identb'], w=[PSK[bt]])
                            yield
                            dtile = ktok if which == 1 else vtok
                            cp('act', dtile[:, tb * 4:tb * 4 + 4, :].rearrange("p a b -> p (a b)"), pb, r=[PSK[bt]],
                               w=[K('gktok' if which == 1 else 'gvtok', tb)])
                            yield
                run_gens([conv_gen(0), conv_gen(1), conv_gen(2)])
                release(mconv)
                fl = lambda t_: t_.rearrange("p a b -> p (a b)")
                bc4 = lambda m_: m_.unsqueeze(1).broadcast_to([128, 4, 128])
                v4 = lambda p_: p_.rearrange("p (a b) -> p a b", a=4)
                mgrp = mark()
                gsets = []
                for si in range(2):
                    d_ = {}
                    for nm_ in ('argL', 'argA', 'EG'):
                        d_[nm_] = alloc(f'g{nm_}{si}', [128, 4, 128])
                    for nm_ in ('dgr_h', 'dgr_l', 'dgg_h', 'dgg_l', 'dgn_h', 'dgn_l'):
                        d_[nm_] = alloc(f'g{nm_}{si}', [128, 4, 128], BF16)
                    for nm_ in ('LT', 'B', 'BT', 'X', 'vb', 'kbg'):
                        d_[nm_] = alloc(f'g{nm_}{si}', [128, 4, 128], BF16)
                    gsets.append(d_)

                def grp_gen(gq, si):
                    t = gsets[si]
                    p0, p1, p2, p3 = [4 * si + i_ for i_ in range(4)]
                    kk = lambda nm_: K('gg' + nm_, si)
                    n0 = gq * 4
                    tb = gq
                    gsl = slice(n0 * 128, (n0 + 4) * 128)
                    colb = lambda nm: sc[nm][:, n0:n0 + 4, h:h + 1].broadcast_to([128, 4, 128])
                    c4 = [slice(c * 128, (c + 1) * 128) for c in range(4)]
                    for dn_, cn_ in (('dgr', 'rr'), ('dgg', 'gc'), ('dgn', 'ngc')):
                        for hl in ('_h', '_l'):
                            tt('dve', t[dn_ + hl], bc4(identb), colb(cn_ + hl), ALU.mult, r=['identb', ka], w=[kk(dn_)])
                    yield
                    mm(PS[p0][:, :], onesb, fl(t['dgg_h']), True, False, r=[kk('dgg'), 'onesb'], w=[PSK[p0]])
                    mm(PS[p0][:, :], onesb, fl(t['dgg_l']), False, True, r=[kk('dgg'), 'onesb'], w=[PSK[p0]])
                    for c in range(4):
                        mm(PS[p1][:, c4[c]], onesb, t['dgr_h'][:, c, :], True, False, r=[kk('dgr'), 'onesb'], w=[PSK[p1]])
                        mm(PS[p1][:, c4[c]], onesb, t['dgr_l'][:, c, :], False, False, r=[kk('dgr'), 'onesb'], w=[PSK[p1]])
                        mm(PS[p1][:, c4[c]], t['dgn_h'][:, c, :], onesb, False, False, r=[kk('dgn'), 'onesb'], w=[PSK[p1]])
                        mm(PS[p1][:, c4[c]], t['dgn_l'][:, c, :], onesb, False, False, r=[kk('dgn'), 'onesb'], w=[PSK[p1]])
                        mm(PS[p1][:, c4[c]], identb, negsu_b, False, True, r=['identb', 'negm'], w=[PSK[p1]])
                    for c in range(4):
                        mm(PS[p2][:, c4[c]], onesb, t['dgg_h'][:, c, :], True, False, r=[kk('dgg'), 'onesb'], w=[PSK[p2]])
                        mm(PS[p2][:, c4[c]], onesb, t['dgg_l'][:, c, :], False, False, r=[kk('dgg'), 'onesb'], w=[PSK[p2]])
                        mm(PS[p2][:, c4[c]], t['dgn_h'][:, c, :], onesb, False, False, r=[kk('dgn'), 'onesb'], w=[PSK[p2]])
                        mm(PS[p2][:, c4[c]], t['dgn_l'][:, c, :], onesb, False, False, r=[kk('dgn'), 'onesb'], w=[PSK[p2]])
                        mm(PS[p2][:, c4[c]], identb, negui_b, False, True, r=['identb', 'negm'], w=[PSK[p2]])
                    yield
                    act(fl(t['EG']), PS[p0][:, :], AF.Exp, r=[PSK[p0]], w=[kk('EG')])
                    act(fl(t['argL']), PS[p1][:, :], AF.Exp, r=[PSK[p1]], w=[kk('argL')])
                    act(fl(t['argA']), PS[p2][:, :], AF.Exp, r=[PSK[p2]], w=[kk('argA')])
                    yield
                    tt('dve', qgT[:, gsl], qT[:, gsl], fl(t['EG']), ALU.mult, r=[K('gq', tb), kk('EG')], w=[K('gqg', gq)])
                    for c in range(4):
                        nsl = slice((n0 + c) * 128, (n0 + c + 1) * 128)
                        mm(PS[p3][:, c4[c]], kT[:, nsl], kT[:, nsl], True, True, r=[K('gk', tb)], w=[PSK[p3]])
                    for c in range(4):
                        nsl = slice((n0 + c) * 128, (n0 + c + 1) * 128)
                        mm(PS[p0][:, c4[c]], kT[:, nsl], qT[:, nsl], True, True, r=[K('gk', tb), K('gq', tb)], w=[PSK[p0]])
                    yield
                    tt('dve', t['LT'], v4(PS[p3][:, :]), t['argL'], ALU.mult, r=[PSK[p3], kk('argL')], w=[kk('LT')])
                    tt('dve', attnT[:, n0:n0 + 4, :], v4(PS[p0][:, :]), t['argA'], ALU.mult, r=[PSK[p0], kk('argA')], w=[K('gattnT', gq)])
                    yield
                    for lb in range(7):
                        mk = bc4(C('lvl', lb * 128, (lb + 1) * 128))
                        Bc = (lambda c: identb) if lb == 0 else (lambda c: t['B'][:, c, :])
                        BTc = (lambda c: identb) if lb == 0 else (lambda c: t['BT'][:, c, :])
                        bkeys = ['identb'] if lb == 0 else [kk('B')]
                        btkeys = ['identb'] if lb == 0 else [kk('BT')]
                        for c in range(4):
                            mm(PS[p1][:, c4[c]], t['LT'][:, c, :], Bc(c), True, True, r=[kk('LT')] + bkeys, w=[PSK[p1]])
                        yield
                        stt('dve', t['X'], v4(PS[p1][:, :]), -1.0, mk, ALU.mult, ALU.mult, r=[PSK[p1], 'cst'], w=[kk('X')])
                        yield
                        if lb < 6:
                            for c in range(4):
                                mm(PS[p2][:, c4[c]], identb, Bc(c), True, False, r=['identb'] + bkeys, w=[PSK[p2]])
                                mm(PS[p2][:, c4[c]], BTc(c), t['X'][:, c, :], False, True, r=btkeys + [kk('X')], w=[PSK[p2]])
                        for c in range(4):
                            mm(PS[p3][:, c4[c]], identb, BTc(c), True, False, r=['identb'] + btkeys, w=[PSK[p3]])
                            mm(PS[p3][:, c4[c]], t['X'][:, c, :], BTc(c), False, True, r=[kk('X')] + btkeys, w=[PSK[p3]])
                        yield
                        if lb < 6:
                            cp('act', fl(t['B']), PS[p2][:, :], r=[PSK[p2]], w=[kk('B')])
                        cp('act', fl(t['BT']), PS[p3][:, :], r=[PSK[p3]], w=[kk('BT')])
                        yield
                    tt('dve', t['vb'], vtok[:, n0:n0 + 4, :], colb('beta'), ALU.mult, r=[K('gvtok', tb), ka], w=[kk('vb')])
                    tt('dve', t['kbg'], ktok[:, n0:n0 + 4, :], colb('bg'), ALU.mult, r=[K('gktok', tb), ka], w=[kk('kbg')])
                    tt('dve', kdtok[:, n0:n0 + 4, :], ktok[:, n0:n0 + 4, :], colb('kdecs'), ALU.mult, r=[K('gktok', tb), ka], w=[K('gkdtok', gq)])
                    yield
                    for c in range(4):
                        mm(PS[p0][:, c4[c]], t['BT'][:, c, :], t['vb'][:, c, :], True, True, r=[kk('BT'), kk('vb')], w=[PSK[p0]])
                    for c in range(4):
                        mm(PS[p1][:, c4[c]], t['kbg'][:, c, :], t['BT'][:, c, :], True, True, r=[kk('BT'), kk('kbg')], w=[PSK[p1]])
                    yield
                    cp('act', fl(U[:, n0:n0 + 4, :]), PS[p0][:, :], r=[PSK[p0]], w=[K('gU', gq)])
                    cp('act', fl(WT[:, n0:n0 + 4, :]), PS[p1][:, :], r=[PSK[p1]], w=[K('gWT', gq)])
                    yield
                run_gens([grp_gen(0, 0), grp_gen(1, 1)])
                run_gens([grp_gen(2, 0), grp_gen(3, 1)])
                release(mgrp)
                Sf = alloc('gSf', [128, 128])
                Sb = alloc('gSb', [128, 128], BF16)
                vnew = alloc('gvnew', [128, 128], BF16)
                hnt = hn_tmps()
                memset('dve', Sf, 0.0, [K('gSf')])
                memset('pool', Sb, 0.0, [K('gSb')])
                for n in range(NT):
                    tb = n // 4
                    nl = n % 4
                    nsl = slice(n * 128, (n + 1) * 128)
                    osl = slice(nl * 128, (nl + 1) * 128)
                    mm(PS[0][:, 0:128], WT[:, n, :], Sb, True, True, r=[K('gWT', n // 4), K('gSb')], w=[PSK[0]])
                    tt('dve', vnew, U[:, n, :], PS[0][:, 0:128], ALU.subtract, r=[K('gU', n // 4), PSK[0]], w=[K('gvnew')])
                    mm(PS[3][:, osl], Sb, qgT[:, nsl], True, False, r=[K('gSb'), K('gqg', n // 4)], w=[PSK[3]])
                    mm(PS[3][:, osl], vnew, attnT[:, n, :], False, True, r=[K('gvnew'), K('gattnT', n // 4)], w=[PSK[3]])
                    mm(PS[1][:, 0:128], kdtok[:, n, :], vnew, True, True, r=[K('gkdtok', n // 4), K('gvnew')], w=[PSK[1]])
                    stt('dve', Sf, Sf, sc['egl'][:, n, h:h + 1], PS[1][:, 0:128], ALU.mult, ALU.add, r=[K('gSf'), PSK[1], ka], w=[K('gSf')])
                    cp('act', Sb, Sf, r=[K('gSf')], w=[K('gSb')])
                    if nl == 3:
                        head_norm_gate(PS[3][:, :], PSK[3], wz, K('gwg_z'), slice(0, 128), tb, PR(l, 'gdng', 0, 1),
                                       oT[:, h, tb * 512:(tb + 1) * 512], K('gdn_oT', tb), hnt)
                release(m1)
            branch_proj(l, 1, oT, None, 4, False)
            release(m0)
        def dil_branch(l):
            m0 = mark()
            odT = alloc('dil_oT', [128, 2, T], BF16)
            numT = alloc('dnum', [128, 2, T], reg='x')
            denT = alloc('dden', [128, 2, T], reg='x')
            tC = alloc('dtC', [128, T])
            tS = alloc('dtS', [128, T])
            S.dma('sp', tC, tab_d[2], r=['tabs'], w=[K('dtC')])
            S.dma('sp', tS, tab_d[3], r=['tabs'], w=[K('dtS')])
            win_v = kp(w_in_d[l])
            for gi, dl in enumerate((1, 4, 16)):
                L = T // dl
                nsub = L // 128
                m1 = mark()

                def pblk(ap2, tb):
                    if dl == 1:
                        return ap2[:, tb * 512:(tb + 1) * 512]
                    v = ap2.rearrange("p (l r) -> p r l", r=dl)
                    if dl == 4:
                        return v[:, tb, :]
                    return v[:, 4 * tb:4 * tb + 4, :]

                def pblk_shape(ap512):
                    if dl == 16:
                        return ap512.rearrange("p (a b) -> p a b", a=4)
                    return ap512

                def psub(ap2, n):
                    if dl == 1:
                        return ap2[:, n * 128:(n + 1) * 128]
                    v = ap2.rearrange("p (l r) -> p r l", r=dl)
                    if dl == 4:
                        return v[:, n // 4, (n % 4) * 128:(n % 4 + 1) * 128]
                    return v[:, n, :]
                wq = alloc('dwq', [128, 8, 256], BF16)
                wk = alloc('dwk', [128, 8, 256], BF16)
                wv = alloc('dwv', [128, 8, 256], BF16)
                for w_, nm in ((wq, 'd_q'), (wk, 'd_k'), (wv, 'd_v')):
                    wload(w_, win_v[:, :, OFF[nm] + gi * 256:OFF[nm] + (gi + 1) * 256], K('dw' + nm))
                qT = alloc('dqT', [128, 2, T], BF16, reg='x')
                kT = alloc('dkT', [128, 2, T], BF16, reg='x')
                vg = alloc('dvg', [128, NT, 256], BF16, reg='x')
                P_ = alloc('dP', [128, 512], BF16)
                Pms = [alloc('dPm0', [128, 512], BF16), alloc('dPm1', [128, 512], BF16)]
                rot_t = (alloc('ro_xq', [128, 512]), alloc('ro_t1', [128, 512]), alloc('ro_t2', [128, 512]))
                for which, (w_, nm) in enumerate(((wq, 'd_q'), (wk, 'd_k'))):
                    for j in range(2):
                        for tb in range(NB):
                            for k in range(8):
                                mm(pblk_shape(PS[0][:, :]), w_[:, k, j * 128:(j + 1) * 128], pblk(xnT[:, k, :], tb), k == 0, k == 7,
                                   r=[K('dw' + nm), ('xnT', 0), ('xnT', 1), ('xnT', 2), ('xnT', 3)], w=[PSK[0]])
                            dst = (qT if which == 0 else kT)[:, j, tb * 512:(tb + 1) * 512]
                            xq, t1, t2 = rot_t
                            if which == 0:
                                S.op('act', lambda e: e.mul(xq, PS[0][:, :], 0.125), r=[PSK[0]], w=[K('ro_xq')])
                            else:
                                cp('act', xq, PS[0][:, :], r=[PSK[0]], w=[K('ro_xq')])
                            xqb = t2.bitcast(BF16)[:, 0:512]
                            cp('act', xqb, xq, r=[K('ro_xq')], w=[K('ro_t2')])
                            mm(PS[5][:, :], permd_b, xqb, True, True, r=[K('ro_t2'), 'negm'], w=[PSK[5]])
                            tt('dve', pblk_shape(t1), pblk_shape(PS[5][:, :]), pblk(tS, tb), ALU.mult, r=[PSK[5], K('dtS')], w=[K('ro_t1')])
                            tt('dve', pblk_shape(t2), pblk_shape(xq), pblk(tC, tb), ALU.mult, r=[K('ro_xq'), K('dtC')], w=[K('ro_t2')])
                            tt('dve', dst, t1, t2, ALU.add, r=[K('ro_t1'), K('ro_t2')], w=[K('dq' if which == 0 else 'dk', j, tb)])
                for n in range(NT):
                    for k in range(8):
                        mm(PS[1][:, 0:256], psub(xnT[:, k, :], n), wv[:, k, :], k == 0, k == 7,
                           r=[K('dwd_v'), ('xnT', 0), ('xnT', 1), ('xnT', 2), ('xnT', 3)], w=[PSK[1]])
                    cp('act', vg[:, n, :], PS[1][:, 0:256], r=[PSK[1]], w=[K('dvg', n)])
                for n in range(NT):
                    nsl = slice(n * 128, (n + 1) * 128)
                    kbs = [(n, 'm_own4')]
                    if n % nsub != 0:
                        kbs.append((n - 1, 'm_prev4'))
                    colh = lambda h_: ((h_ % 2) * 2 + h_ // 2) * 128
                    for bi, (kb, mname) in enumerate(kbs):
                        ksl = slice(kb * 128, (kb + 1) * 128)
                        banks = (2, 6) if bi == 0 else (5, 7)
                        for h in range(4):
                            rows = slice((h % 2) * 64, (h % 2 + 1) * 64)
                            bk = banks[h % 2]
                            mm(PS[bk][:, (h // 2) * 128:(h // 2 + 1) * 128], kT[rows, h // 2, ksl], qT[rows, h // 2, nsl], True, False,
                               r=[K('dk', h // 2, kb // 4), K('dq', h // 2, n // 4)], w=[PSK[bk]])
                            mm(PS[bk][:, (h // 2) * 128:(h // 2 + 1) * 128], identb, negm[0 if mname == 'm_own4' else 1], False, True,
                               r=['identb', 'negm'], w=[PSK[bk]])
                        for par in range(2):
                            act(Pms[bi][:, par * 256:(par + 1) * 256], PS[banks[par]][:, 0:256], AF.Exp, r=[PSK[banks[par]]], w=[K('dPm', bi)])
                    nkb = len(kbs)
                    for h in range(4):
                        for bi, (kb, mname) in enumerate(kbs):
                            mm(PS[3][:, h * 128:(h + 1) * 128], vg[:, kb, (h // 2) * 128:(h // 2 + 1) * 128], Pms[bi][:, colh(h):colh(h) + 128],
                               bi == 0, bi == nkb - 1, r=[K('dvg', kb), K('dPm', bi)], w=[PSK[3]])
                    for bi, (kb, mname) in enumerate(kbs):
                        mm(PS[4][:, :], onesb, Pms[bi], bi == 0, bi == nkb - 1, r=['onesb', K('dPm', bi)], w=[PSK[4]])
                    for h in range(4):
                        rows = slice((h % 2) * 64, (h % 2 + 1) * 64)
                        dn = psub(numT[rows, h // 2, :], n)
                        dd = psub(denT[rows, h // 2, :], n)
                        if gi == 0:
                            cp('act', dn, PS[3][rows, h * 128:(h + 1) * 128], r=[PSK[3]], w=[K('dnum')])
                            cp('dve', dd, PS[4][rows, colh(h):colh(h) + 128], r=[PSK[4]], w=[K('dden')])
                        else:
                            tt('dve', dn, dn, PS[3][rows, h * 128:(h + 1) * 128], ALU.add, r=[PSK[3], K('dnum')], w=[K('dnum')])
                            tt('dve', dd, dd, PS[4][rows, colh(h):colh(h) + 128], ALU.add, r=[PSK[4], K('dden')], w=[K('dden')])
                release(m1)
            for j in range(2):
                recip(denT[:, j, :], denT[:, j, :], r=[], w=[])
            S.barrier()
            for j in range(2):
                tt('dve', odT[:, j, :], numT[:, j, :], denT[:, j, :], ALU.mult, r=[], w=[])
            S.barrier()
            branch_proj(l, 2, odT, None, 2, False)
            release(m0)

        def mixer(l):
            rmsnorm_xn(PR(l, 'g_mix', 0, 8))
            S.barrier()
            for c_ in range(8):
                S.dma('sp', xsp_d[:, c_ * T:(c_ + 1) * T], xT[:, c_, :], r=[], w=['xsp'])
            S.barrier()
            m = mark()
            H['merged'] = alloc('merged', [128, 8, T], BF16)
            if 'r' in phases:
                ret_branch(l)
            else:
                for c_ in range(8):
                    memset('pool', H['merged'][:, c_, :], 0.0, [('merged', c_, t_) for t_ in range(4)])
            if 'g' in phases:
                gdn_branch(l)
            if 'd' in phases:
                dil_branch(l)
            S.barrier()
            for c_ in range(8):
                S.dma('sp', xT[:, c_, :], xsp_d[:, c_ * T:(c_ + 1) * T], r=['xsp'], w=[])
            S.barrier()
            wo_ = [alloc(f'wout{i}', [128, 8, 128], BF16) for i in range(2)]
            wv_ = kp(w_out_d[l])
            mg = H['merged']
            for c in range(8):
                i = c % 2
                wload(wo_[i], wv_[:, :, c * 128:(c + 1) * 128], K('wout', i))
                for tb in range(NB):
                    sl = slice(tb * 512, (tb + 1) * 512)
                    for k in range(8):
                        mm(PS[c % 2][:, :], wo_[i][:, k, :], mg[:, k, sl], k == 0, k == 7, r=[K('wout', i)], w=[PSK[c % 2]])
                    tt('dve', xT[:, c, sl], xT[:, c, sl], PS[c % 2][:, :], ALU.add, r=[PSK[c % 2], ('xT', c, tb)], w=[('xT', c, tb)])
            release(m)

        def xattn(l):
            rmsnorm_xn(PR(l, 'g_xat', 0, 8))
            m = mark()
            wq = alloc('xwq', [128, 8, 512], BF16)
            wkv = alloc('xwkv', [128, 8, 1024], BF16)
            wo = alloc('xwo', [128, 4, 1024], BF16)
            for k_ in range(8):
                wload(wq[:, k_, :], kp(wq_d[l])[:, k_, :], K('xwq'))
                wload(wkv[:, k_, :], kp(wkv_d[l])[:, k_, :], K('xwkv'))
            for k_ in range(4):
                wload(wo[:, k_, :], kp(wo_d[l])[:, k_, :], K('xwo'))
            memn = alloc('xmemn', [128, 8, 256], BF16)
            for c in range(8):
                ts('dve', memn[:, c, :], memh[:, c, :], PR(l, 'g_mem', c, c + 1), None, ALU.mult, None, r=['memh', 'prm'], w=[K('xmemn')])
            KT = alloc('xKT', [128, 4, 256], BF16)
            V = alloc('xV', [128, 2, 512], BF16)
            for h in range(4):
                for k in range(8):
                    mm(PS[0][:, 0:256], wkv[:, k, h * 128:(h + 1) * 128], memn[:, k, :], k == 0, k == 7, r=[K('xwkv'), K('xmemn')], w=[PSK[0]])
                cp('act', KT[:, h, :], PS[0][:, 0:256], r=[PSK[0]], w=[K('xKT')])
            for mt in range(2):
                for k in range(8):
                    mm(PS[1][:, :], memn[:, k, mt * 128:(mt + 1) * 128], wkv[:, k, 512:1024], k == 0, k == 7, r=[K('xwkv'), K('xmemn')], w=[PSK[1]])
                cp('act', V[:, mt, :], PS[1][:, :], r=[PSK[1]], w=[K('xV')])
            QT = alloc('xQT', [128, 512], BF16)
            Pq = alloc('xP', [128, 2, 512], BF16)
            rd = alloc('xrd', [128, 512])
            ox = alloc('xox', [128, 4, 512], BF16)
            for tb in range(NB):
                sl = slice(tb * 512, (tb + 1) * 512)
                for h in range(4):
                    projF(PS[0][:, :], PSK[0], wq, K('xwq'), slice(h * 128, (h + 1) * 128), xn_blk(tb), [('xnT', tb)])
                    S.op('act', lambda e: e.mul(QT, PS[0][:, :], float(128 ** -0.5)), r=[PSK[0]], w=[K('xQT')])
                    for mt in range(2):
                        mm(PS[1 + mt][:, :], KT[:, h, mt * 128:(mt + 1) * 128], QT, True, True, r=[K('xKT'), K('xQT')], w=[PSK[1 + mt]])
                        act(Pq[:, mt, :], PS[1 + mt][:, :], AF.Exp, r=[PSK[1 + mt]], w=[K('xP', mt)])
                    for mt in range(2):
                        mm(PS[3][:, :], V[:, mt, h * 128:(h + 1) * 128], Pq[:, mt, :], mt == 0, mt == 1, r=[K('xV'), K('xP', mt)], w=[PSK[3]])
                    for mt in range(2):
                        mm(PS[4][:, :], onesb, Pq[:, mt, :], mt == 0, mt == 1, r=['onesb', K('xP', mt)], w=[PSK[4]])
                    recip(rd, PS[4][:, :], r=[PSK[4]], w=[K('xrd')])
                    tt('dve', ox[:, h, :], PS[3][:, :], rd, ALU.mult, r=[PSK[3], K('xrd')], w=[K('xox', h)])
                for c in range(8):
                    for h in range(4):
                        mm(PS[5 + c % 2][:, :], wo[:, h, c * 128:(c + 1) * 128], ox[:, h, :], h == 0, h == 3, r=[K('xwo'), K('xox', h)], w=[PSK[5 + c % 2]])
                    tt('dve', xT[:, c, sl], xT[:, c, sl], PS[5 + c % 2][:, :], ALU.add, r=[PSK[5 + c % 2], ('xT', c, tb)], w=[('xT', c, tb)])
            release(m)

        def ffn(l):
            rmsnorm_xn(PR(l, 'g_ffn', 0, 8))
            m = mark()
            hT = alloc('fhT', [128, 11, T], BF16)
            wa = [alloc(f'fwa{i}', [128, 8, 128], BF16) for i in range(2)]
            wu = [alloc(f'fwu{i}', [128, 8, 128], BF16) for i in range(2)]
            wd = [alloc(f'fwd{i}', [128, 11, 128], BF16) for i in range(2)]
            rawa = [alloc(f'frawa{i}', [128, 514]) for i in range(2)]
            rawu = [alloc(f'frawu{i}', [128, 514]) for i in range(2)]
            accas = [alloc(f'facca{i}', [128, 512]) for i in range(2)]
            accus = [alloc(f'faccu{i}', [128, 512]) for i in range(2)]
            sas = [alloc(f'fsa{i}', [128, 512]) for i in range(2)]
            ptmp = alloc('fptmp', [128, 512])
            wup_v = kp(wup_d[l])
            wdn_v = kp(wdn_d[l])
            for half in range(2):
                for jl in range(11):
                    jj = half * 11 + jl
                    i = jl % 2
                    wload(wa[i], wup_v[:, :, jj * 128:(jj + 1) * 128], K('fwa', i))
                    wload(wu[i], wup_v[:, :, 2816 + jj * 128:2816 + (jj + 1) * 128], K('fwu', i))
                    for tb in range(NB):
                        sl = slice(tb * 512, (tb + 1) * 512)
                        info = []
                        acca, accu, sa = accas[tb % 2], accus[tb % 2], sas[tb % 2]
                        for which in range(2):
                            w_, wk_ = (wa[i], K('fwa', i)) if which == 0 else (wu[i], K('fwu', i))
                            raws = rawa if which == 0 else rawu
                            bk = which + 2 * (tb % 2)
                            info.append((raws, raws[tb % 2], K('fraw', which, tb % 2), jj if which == 0 else 22 + jj, bk))
                            projF(PS[bk][:, :], PSK[bk], w_, wk_, slice(0, 128), xn_blk(tb), [('xnT', tb)])
                        for which in range(2):
                            raws, rw, rk, cc_, bk = info[which]
                            if tb == 0:
                                memset('dve', rw[:, 0:2], 0.0, [rk])
                            else:
                                cp('act', rw[:, 0:2], raws[(tb - 1) % 2][:, 512:514], r=[K('fraw', which, (tb - 1) % 2)], w=[rk])
                            cp('act', rw[:, 2:514], PS[bk][:, :], r=[PSK[bk]], w=[rk])
                        for which in range(2):
                            raws, rw, rk, cc_, bk = info[which]
                            ac = acca if which == 0 else accu
                            ak = K('facc', which, tb % 2)
                            ts('dve', ac, rw[:, 2:514], PR(l, 'fconv', cc_ * 3 + 2, cc_ * 3 + 3), None, ALU.mult, None, r=[rk, 'prm'], w=[ak])
                            for t_ in range(0, 2):
                                stt('dve', ac, rw[:, t_:t_ + 512], PR(l, 'fconv', cc_ * 3 + t_, cc_ * 3 + t_ + 1), ac, ALU.mult, ALU.add,
                                    r=[rk, 'prm', ak], w=[ak])
                            if which == 0:
                                act(sa, ac, AF.Silu, r=[ak], w=[K('fsa', tb % 2)], bias=PR(l, 'fbias', cc_, cc_ + 1), scale=1.0)
                        raws, rw, rk, cc_, bk = info[1]
                        stt('dve', hT[:, jl, sl], accu, PR(l, 'fbias', cc_, cc_ + 1), sa, ALU.add, ALU.mult,
                            r=[K('facc', 1, tb % 2), K('fsa', tb % 2), 'prm'], w=[K('fhT', jl, tb)])
                for c in range(8):
                    i = c % 2
                    wload(wd[i], wdn_v[:, half * 11:(half + 1) * 11, c * 128:(c + 1) * 128], K('fwd', i))
                    for tb in range(NB):
                        sl = slice(tb * 512, (tb + 1) * 512)
                        for k in range(11):
                            mm(PS[4 + c % 2][:, :], wd[i][:, k, :], hT[:, k, sl], k == 0, k == 10, r=[K('fwd', i), K('fhT', k, tb)], w=[PSK[4 + c % 2]])
                        tt('dve', xT[:, c, sl], xT[:, c, sl], PS[4 + c % 2][:, :], ALU.add, r=[PSK[4 + c % 2], ('xT', c, tb)], w=[('xT', c, tb)])
            release(m)

        def seq_setup(s):
            m = mark()
            xin = [alloc(f'xin{i}', [128, DM]) for i in range(2)]
            for n in range(NT):
                i = n % 2
                S.dma('sp', xin[i], x_d[s, n * 128:(n + 1) * 128, :], r=[], w=[K('xin', i)])
                for half in range(2):
                    for cq in range(4):
                        c = half * 4 + cq
                        mm(PS[half][:, cq * 128:(cq + 1) * 128], xin[i][:, c * 128:(c + 1) * 128], identf, True, True, r=[K('xin', i), 'cst'], w=[PSK[half]])
                    cp('act' if half == 0 else 'dve', xT[:, half * 4:half * 4 + 4, n * 128:(n + 1) * 128],
                       PS[half][:, :].rearrange("p (a b) -> p a b", a=4), r=[PSK[half]],
                       w=[('xT', half * 4 + q_, n // 4) for q_ in range(4)])
            pi_ = alloc('pos_i', [128, T], I32)
            pf = alloc('pos_f', [128, T])
            u = alloc('tab_u', [128, T])
            ui = alloc('tab_ui', [128, T], I32)
            uf = alloc('tab_uf', [128, T])
            S.dma('sp', pi_, pos_d[s:s + 1, :].broadcast_to([128, T]), r=[], w=[K('pos_i')])
            cp('dve', pf, pi_, r=[K('pos_i')], w=[K('pos_f')])
            for ti in range(4):
                which = ti // 2
                iscos = (ti % 2 == 0)
                ts('dve', u, pf, C('invf', which, which + 1), float(1.0 / (2 * np.pi)), ALU.mult, ALU.mult, r=[K('pos_f'), 'cst'], w=[K('tab_u')])
                if iscos:
                    ts('dve', u, u, 0.25, None, ALU.add, None, r=[K('tab_u')], w=[K('tab_u')])
                cp('dve', ui, u, r=[K('tab_u')], w=[K('tab_ui')])
                cp('dve', uf, ui, r=[K('tab_ui')], w=[K('tab_uf')])
                tt('dve', u, u, uf, ALU.subtract, r=[K('tab_u'), K('tab_uf')], w=[K('tab_u')])
                stt('dve', uf, u, 0.5, u, ALU.is_gt, ALU.subtract, r=[K('tab_u')], w=[K('tab_uf')])
                act(uf, uf, AF.Sin, r=[K('tab_uf')], w=[K('tab_uf')], scale=float(-2 * np.pi))
                S.dma('sp', tab_d[ti], uf, r=[K('tab_uf')], w=['tabs'])
            mi = alloc('mem_in', [128, DM])
            junk = alloc('mem_junk', [128, DM])
            ss = alloc('mem_ss', [128, 1])
            for mt in range(2):
                S.dma('sp', mi, mem_d[s, mt * 128:(mt + 1) * 128, :], r=[], w=[K('mem_in')])
                act(junk, mi, AF.Square, r=[K('mem_in')], w=[K('mem_junk'), K('mem_ss')], accum_out=ss)
                act(ss, ss, AF.Sqrt, r=[K('mem_ss')], w=[K('mem_ss')], bias=EPS, scale=1.0 / DM)
                recip(ss, ss, r=[K('mem_ss')], w=[K('mem_ss')])
                ts('dve', mi, mi, ss[:, 0:1], None, ALU.mult, None, r=[K('mem_in'), K('mem_ss')], w=[K('mem_in')])
                for half in range(2):
                    for cq in range(4):
                        c = half * 4 + cq
                        mm(PS[2 + half][:, cq * 128:(cq + 1) * 128], mi[:, c * 128:(c + 1) * 128], identf, True, True, r=[K('mem_in'), 'cst'], w=[PSK[2 + half]])
                    cp('act', memh[:, half * 4:half * 4 + 4, mt * 128:(mt + 1) * 128],
                       PS[2 + half][:, :].rearrange("p (a b) -> p a b", a=4), r=[PSK[2 + half]], w=['memh'])
            release(m)

        def final_out(s):
            m = mark()
            sq = alloc('fo_sq', [128, 2, 512], BF16)
            rs = alloc('fo_rs', [128, 512])
            xo = alloc('fo_xo', [128, 8, 512])
            ot = [alloc(f'fo_ot{i}', [128, DM]) for i in range(2)]
            gcols = prm[:, 2 * PRM_L:2 * PRM_L + 8]
            for tb in range(NB):
                sl = slice(tb * 512, (tb + 1) * 512)
                for c in range(8):
                    act(sq[:, c % 2, :], xT[:, c, sl], AF.Square, r=[('xT', c, tb)], w=[K('fo_sq', c % 2)])
                    mm(PS[0][:, :], onesb, sq[:, c % 2, :], c == 0, c == 7, r=[K('fo_sq', c % 2), 'onesb'], w=[PSK[0]])
                act(rs, PS[0][:, :], AF.Sqrt, r=[PSK[0]], w=[K('fo_rs')], bias=EPS, scale=1.0 / DM)
                recip(rs, rs, r=[K('fo_rs')], w=[K('fo_rs')])
                for c in range(8):
                    stt('dve', xo[:, c, :], xT[:, c, sl], gcols[:, c:c + 1], rs, ALU.mult, ALU.mult,
                        r=[('xT', c, tb), K('fo_rs'), 'prm'], w=[K('fo_xo')])
                for nl in range(4):
                    n = tb * 4 + nl
                    i = n % 2
                    for half in range(2):
                        for cq in range(4):
                            c = half * 4 + cq
                            mm(PS[1 + half][:, cq * 128:(cq + 1) * 128], xo[:, c, nl * 128:(nl + 1) * 128], identf, True, True, r=[K('fo_xo'), 'cst'], w=[PSK[1 + half]])
                        cp('act' if half == 0 else 'dve', ot[i][:, half * 512:(half + 1) * 512], PS[1 + half][:, :], r=[PSK[1 + half]], w=[K('fo_ot', i)])
                    S.dma('sp', out_d[s, n * 128:(n + 1) * 128, :], ot[i], r=[K('fo_ot', i)], w=[])
            release(m)

        H = {}
        for s in range(nseq):
            seq_setup(s)
            for l in range(nlayer):
                if any(c_ in phases for c_ in 'rgdo'):
                    mixer(l)
                if 'x' in phases:
                    xattn(l)
                if 'f' in phases:
                    ffn(l)
            final_out(s)
        S.finish()
        print("instructions", S.ninst, "waits", S.nwaits, "arena top", astate['top'])
    return nc


_NC_CACHE = {}


def kernel(**inputs):
    inp = {k: np.asarray(v) for k, v in inputs.items()}
    if 'nc' not in _NC_CACHE:
        _NC_CACHE['nc'] = build(2, 2)
    nc = _NC_CACHE['nc']
    consts = make_consts()
    prm = make_prm(inp)
    shared = {k: np.ascontiguousarray(inp[k], dtype=np.float32) for k in
              ('w_in', 'w_br_ret', 'w_br_gdn', 'w_br_dil', 'w_out', 'xattn_wq', 'xattn_wkv', 'xattn_wo', 'ffn_w_up', 'ffn_w_down')}
    in_maps = []
    for c in range(8):
        d = dict(shared)
        d['x'] = np.ascontiguousarray(inp['x'][2 * c:2 * c + 2], dtype=np.float32)
        d['mem'] = np.ascontiguousarray(inp['mem'][2 * c:2 * c + 2], dtype=np.float32)
        d['positions'] = np.ascontiguousarray(inp['positions'][2 * c:2 * c + 2], dtype=np.int32)
        d['consts'] = consts
        d['prm'] = prm
        in_maps.append(d)
    res = run_bass_kernel_spmd(nc, in_maps, core_ids=list(range(8)))
    return np.concatenate([r['out'] for r in res.results], axis=0).astype(np.float32)
```
